# Optimizing a Trainium2 kernel written in Bass

```python
import math
import jax
import jax.numpy as jnp
from jax import lax
import numpy as np

D_MODEL = 1024
BATCH = 2
SEQ = 16384
DEPTH = 4

N_A_LAYERS = DEPTH // 2
N_B_LAYERS = DEPTH - N_A_LAYERS
EPS = 1e-6

GDN_HEADS = 6
GDN_DK = 128
GDN_DV = 128
GDN_QK_W = GDN_HEADS * GDN_DK
GDN_V_W = GDN_HEADS * GDN_DV
CONV_K = 4
CHUNK = 64

SWA_HEADS = 12
SWA_KV_HEADS = 2
SWA_DH = 64
SWA_GROUP = SWA_HEADS // SWA_KV_HEADS
SWA_Q_W = SWA_HEADS * SWA_DH
KV_W = SWA_KV_HEADS * SWA_DH
WINDOW = 128
SWA_BLOCK = 128
ROPE_THETA = 500000.0
ROT_DIM = SWA_DH // 4

MEM_LEN = 256
MEM_HEADS = 4
MEM_DH = 64
MEM_W = MEM_HEADS * MEM_DH

D_MIX = GDN_V_W + MEM_W
GDN_IN = 2 * GDN_QK_W + 2 * GDN_V_W + 2 * GDN_HEADS + MEM_W
SWA_IN = SWA_Q_W + MEM_W
D_FF = -(-8 * D_MODEL // (3 * 256)) * 256

kernel_name = "yoco_gdn_swa_sink_memory_trunk"


def rms_norm(x, g):
    xf = x.astype(jnp.float32)
    y = xf * lax.rsqrt(jnp.mean(xf * xf, axis=-1, keepdims=True) + EPS)
    return (y * g.astype(jnp.float32)).astype(x.dtype)


def l2_normalize(x):
    xf = x.astype(jnp.float32)
    return xf * lax.rsqrt(jnp.sum(xf * xf, axis=-1, keepdims=True) + EPS)


def rope_tables(positions):
    inv = ROPE_THETA ** (-jnp.arange(0, ROT_DIM, 2, dtype=jnp.float32) / ROT_DIM)
    ang = positions.astype(jnp.float32)[..., None] * inv
    return jnp.cos(ang), jnp.sin(ang)


def apply_partial_rope(x, cos, sin):
    half = ROT_DIM // 2
    xf = x.astype(jnp.float32)
    x1, x2 = xf[..., :half], xf[..., half:ROT_DIM]
    c, s = cos[:, :, None, :], sin[:, :, None, :]
    out = jnp.concatenate([x1 * c - x2 * s, x2 * c + x1 * s, xf[..., ROT_DIM:]], axis=-1)
    return out.astype(x.dtype)


def causal_depthwise_conv(x, w):
    c = x.shape[-1]
    return lax.conv_general_dilated(
        x, w[:, None, :].astype(x.dtype), window_strides=(1,), padding=[(CONV_K - 1, 0)],
        dimension_numbers=("NWC", "WIO", "NWC"), feature_group_count=c)


def swiglu(h, w_gate_up, w_down):
    gu = h @ w_gate_up
    return (jax.nn.silu(gu[..., :D_FF]) * gu[..., D_FF:]) @ w_down


def gated_delta_rule_chunked(q, k, v, g, beta):
    b_sz, s_len, n_h, dk = q.shape
    dv = v.shape[-1]
    n_ch = s_len // CHUNK

    def chunks(t):
        t = t.reshape((b_sz, n_ch, CHUNK, n_h) + t.shape[3:])
        return jnp.moveaxis(t, 3, 1)

    q = chunks(q) * (dk ** -0.5)
    k = chunks(k)
    v = chunks(v)
    beta = chunks(beta)
    gc = jnp.cumsum(chunks(g), axis=-1)
    tril = jnp.tril(jnp.ones((CHUNK, CHUNK), dtype=bool))
    strict = jnp.tril(jnp.ones((CHUNK, CHUNK), dtype=bool), -1)
    decay = jnp.exp(jnp.where(tril, gc[..., :, None] - gc[..., None, :], -jnp.inf))
    kb = k * beta[..., None]
    lower = jnp.where(strict, jnp.einsum("bhncd,bhnkd->bhnck", kb, k) * decay, 0.0)
    rhs = jnp.concatenate([v * beta[..., None], kb * jnp.exp(gc)[..., None]], axis=-1)
    sol = lax.linalg.triangular_solve(lower, rhs, left_side=True, lower=True, unit_diagonal=True)
    u, w = sol[..., :dv], sol[..., dv:]
    intra = jnp.einsum("bhncd,bhnkd->bhnck", q, k) * decay
    q_g = q * jnp.exp(gc)[..., None]
    k_g = k * jnp.exp(gc[..., -1:] - gc)[..., None]
    g_last = jnp.exp(gc[..., -1])
    xs = tuple(jnp.moveaxis(t, 2, 0) for t in (u, w, intra, q_g, k_g, g_last))

    def step(state, inp):
        u_n, w_n, a_n, qg_n, kg_n, gl_n = inp
        v_new = u_n - jnp.einsum("bhck,bhkv->bhcv", w_n, state)
        o_n = jnp.einsum("bhck,bhkv->bhcv", qg_n, state) + jnp.einsum("bhcs,bhsv->bhcv", a_n, v_new)
        state = state * gl_n[..., None, None] + jnp.einsum("bhck,bhcv->bhkv", kg_n, v_new)
        return state, o_n

    s0 = jnp.zeros((b_sz, n_h, dk, dv), jnp.float32)
    _, o = lax.scan(step, s0, xs)
    return jnp.transpose(o, (1, 0, 3, 2, 4)).reshape(b_sz, s_len, n_h, dv)


def gated_deltanet_mixer(h, w_in, conv_w, a_log, dt_bias, norm_g):
    b_sz, s_len, _ = h.shape
    proj = h @ w_in
    o1 = 2 * GDN_QK_W + GDN_V_W
    qkv = jax.nn.silu(causal_depthwise_conv(proj[..., :o1], conv_w))
    z = proj[..., o1:o1 + GDN_V_W]
    o2 = o1 + GDN_V_W
    b_logit = proj[..., o2:o2 + GDN_HEADS].astype(jnp.float32)
    a_logit = proj[..., o2 + GDN_HEADS:o2 + 2 * GDN_HEADS].astype(jnp.float32)
    mem_q = proj[..., o2 + 2 * GDN_HEADS:]
    q = l2_normalize(qkv[..., :GDN_QK_W].reshape(b_sz, s_len, GDN_HEADS, GDN_DK))
    k = l2_normalize(qkv[..., GDN_QK_W:2 * GDN_QK_W].reshape(b_sz, s_len, GDN_HEADS, GDN_DK))
    v = qkv[..., 2 * GDN_QK_W:].reshape(b_sz, s_len, GDN_HEADS, GDN_DV).astype(jnp.float32)
    beta = jax.nn.sigmoid(b_logit)
    g = -jnp.exp(a_log.astype(jnp.float32)) * jax.nn.softplus(a_logit + dt_bias.astype(jnp.float32))
    o = gated_delta_rule_chunked(q, k, v, g, beta)
    o = rms_norm(o, norm_g) * jax.nn.silu(z.reshape(b_sz, s_len, GDN_HEADS, GDN_DV).astype(jnp.float32))
    return o.reshape(b_sz, s_len, GDN_V_W).astype(h.dtype), mem_q


def sliding_window_attention(q, k, v, sinks):
    b_sz, s_len = q.shape[:2]
    nb = s_len // SWA_BLOCK
    qb = q.reshape(b_sz, nb, SWA_BLOCK, SWA_KV_HEADS, SWA_GROUP, SWA_DH)
    kb = k.reshape(b_sz, nb, SWA_BLOCK, SWA_KV_HEADS, SWA_DH)
    vb = v.reshape(b_sz, nb, SWA_BLOCK, SWA_KV_HEADS, SWA_DH)
    pad = jnp.zeros_like(kb[:, :1])
    kw = jnp.concatenate([jnp.concatenate([pad, kb[:, :-1]], axis=1), kb], axis=2)
    vw = jnp.concatenate([jnp.concatenate([pad, vb[:, :-1]], axis=1), vb], axis=2)
    s = jnp.einsum("bnqhgd,bnkhd->bnhgqk", qb, kw).astype(jnp.float32) * (SWA_DH ** -0.5)
    qi = jnp.arange(SWA_BLOCK)[:, None] + SWA_BLOCK
    ki = jnp.arange(2 * SWA_BLOCK)[None, :]
    diff = qi - ki
    band = (diff >= 0) & (diff < WINDOW)
    has_prev = jnp.arange(nb) > 0
    mask = band[None] & (has_prev[:, None, None] | (ki >= SWA_BLOCK)[None])
    s = jnp.where(mask[None, :, None, None], s, -jnp.inf)
    sink = sinks.astype(jnp.float32).reshape(SWA_KV_HEADS, SWA_GROUP)[None, None, :, :, None, None]
    m = jnp.maximum(jnp.max(s, axis=-1, keepdims=True), sink)
    p = jnp.exp(s - m)
    p = (p / (jnp.sum(p, axis=-1, keepdims=True) + jnp.exp(sink - m))).astype(v.dtype)
    o = jnp.einsum("bnhgqk,bnkhd->bnqhgd", p, vw)
    return o.reshape(b_sz, s_len, SWA_Q_W)


def memory_attention(q, mem_k, mem_v):
    s = jnp.einsum("bshd,bmhd->bhsm", q, mem_k).astype(jnp.float32) * (MEM_DH ** -0.5)
    p = jax.nn.softmax(s, axis=-1).astype(q.dtype)
    o = jnp.einsum("bhsm,bmhd->bshd", p, mem_v)
    return o.reshape(q.shape[0], q.shape[1], MEM_W)


def setup_inputs(seed: int = 0) -> dict:
    key = jax.random.key(seed)
    ks = jax.random.split(key, 24)
    f32 = jnp.float32

    def dense(k, shape, fan_in):
        return jax.random.normal(k, shape, f32) * (fan_in ** -0.5)

    def gain(k, shape):
        return 1.0 + 0.02 * jax.random.normal(k, shape, f32)

    x = jax.random.normal(ks[0], (BATCH, SEQ, D_MODEL), f32)
    mem = jax.random.normal(ks[1], (BATCH, MEM_LEN, D_MODEL), f32)
    positions = (jnp.arange(SEQ, dtype=jnp.int32)[None, :]
                 + jax.random.randint(ks[2], (BATCH, 1), 0, 4096, dtype=jnp.int32))
    dt0 = jnp.exp(jax.random.uniform(ks[15], (N_A_LAYERS, GDN_HEADS), f32,
                                     math.log(1e-3), math.log(1e-1)))
    return {
        "x": x,
        "mem": mem,
        "positions": positions,
        "ln_mix": gain(ks[3], (DEPTH, D_MODEL)),
        "ln_ffn": gain(ks[4], (DEPTH, D_MODEL)),
        "ln_mem": gain(ks[5], (D_MODEL,)),
        "w_mem_kv": dense(ks[6], (DEPTH, D_MODEL, 2 * MEM_W), D_MODEL),
        "w_out": dense(ks[7], (DEPTH, D_MIX, D_MODEL), D_MIX),
        "w_gate_up": dense(ks[8], (DEPTH, D_MODEL, 2 * D_FF), D_MODEL),
        "w_down": dense(ks[9], (DEPTH, D_FF, D_MODEL), D_FF),
        "gdn_w_in": dense(ks[10], (N_A_LAYERS, D_MODEL, GDN_IN), D_MODEL),
        "gdn_conv": dense(ks[11], (N_A_LAYERS, CONV_K, 2 * GDN_QK_W + GDN_V_W), CONV_K),
        "gdn_A_log": jnp.log(jax.random.uniform(ks[12], (N_A_LAYERS, GDN_HEADS), f32, 1.0, 16.0)),
        "gdn_dt_bias": dt0 + jnp.log(-jnp.expm1(-dt0)),
        "gdn_norm": gain(ks[13], (N_A_LAYERS, GDN_DV)),
        "swa_w_q": dense(ks[14], (N_B_LAYERS, D_MODEL, SWA_IN), D_MODEL),
        "swa_sinks": 0.5 * jax.random.normal(ks[16], (N_B_LAYERS, SWA_HEADS), f32),
        "ln_kv": gain(ks[17], (D_MODEL,)),
        "w_kv": dense(ks[18], (D_MODEL, 2 * KV_W), D_MODEL),
        "ln_final": gain(ks[19], (D_MODEL,)),
    }


def reference(x, mem, positions, ln_mix, ln_ffn, ln_mem, w_mem_kv, w_out, w_gate_up, w_down,
              gdn_w_in, gdn_conv, gdn_A_log, gdn_dt_bias, gdn_norm,
              swa_w_q, swa_sinks, ln_kv, w_kv, ln_final):
    b_sz, s_len, _ = x.shape
    cos, sin = rope_tables(positions)
    mem_n = rms_norm(mem, ln_mem)
    shared_k = None
    shared_v = None
    for layer in range(DEPTH):
        h = rms_norm(x, ln_mix[layer])
        mkv = mem_n @ w_mem_kv[layer]
        mem_k = mkv[..., :MEM_W].reshape(b_sz, MEM_LEN, MEM_HEADS, MEM_DH)
        mem_v = mkv[..., MEM_W:].reshape(b_sz, MEM_LEN, MEM_HEADS, MEM_DH)
        if layer < N_A_LAYERS:
            a = layer
            mix_out, mem_q = gated_deltanet_mixer(h, gdn_w_in[a], gdn_conv[a], gdn_A_log[a],
                                                  gdn_dt_bias[a], gdn_norm[a])
        else:
            bl = layer - N_A_LAYERS
            proj = h @ swa_w_q[bl]
            q = apply_partial_rope(proj[..., :SWA_Q_W].reshape(b_sz, s_len, SWA_HEADS, SWA_DH), cos, sin)
            mix_out = sliding_window_attention(q, shared_k, shared_v, swa_sinks[bl])
            mem_q = proj[..., SWA_Q_W:]
        mem_o = memory_attention(mem_q.reshape(b_sz, s_len, MEM_HEADS, MEM_DH), mem_k, mem_v)
        x = x + jnp.concatenate([mix_out.astype(x.dtype), mem_o.astype(x.dtype)], axis=-1) @ w_out[layer]
        x = x + swiglu(rms_norm(x, ln_ffn[layer]), w_gate_up[layer], w_down[layer])
        if layer == N_A_LAYERS - 1:
            kv = rms_norm(x, ln_kv) @ w_kv
            shared_k = apply_partial_rope(kv[..., :KV_W].reshape(b_sz, s_len, SWA_KV_HEADS, SWA_DH), cos, sin)
            shared_v = kv[..., KV_W:].reshape(b_sz, s_len, SWA_KV_HEADS, SWA_DH)
    return rms_norm(x, ln_final)
```

```python
import contextlib
import numpy as np
import concourse.bass as bass
import concourse.mybir as mybir
from concourse.bass_utils import run_bass_kernel_spmd

F32 = mybir.dt.float32
BF16 = mybir.dt.bfloat16
I32 = mybir.dt.int32
AF = mybir.ActivationFunctionType
ALU = mybir.AluOpType
AX = mybir.AxisListType

D = 1024
KC = 8
DFF = 2816
FC = 22
GDN_IN = 3340
EPS = 1e-6
BIG = 1.0e30
MEM_LEN = 256

COMPUTE = ("pe", "act", "dve", "pool")
ALLENG = COMPUTE + ("sp",)
N_DMA_SEMS = 64


class Buf:
    __slots__ = ("w", "r", "excl")

    def __init__(self, excl=False):
        self.w = None
        self.r = []
        self.excl = excl


class TL:
    def __init__(self, ap, nsub=1, excl=False):
        self.ap = ap
        if excl:
            b = Buf(True)
            self.bufs = [b] * nsub
        else:
            self.bufs = [Buf() for _ in range(nsub)]

    def b(self, i=None):
        if i is None:
            return list(self.bufs)
        return [self.bufs[i]]


class Prog:
    def __init__(self):
        self.ops = {e: [] for e in ALLENG}
        self.cnt = {e: 0 for e in COMPUTE}
        self.seen = {e: {} for e in ALLENG}
        self.pending = {e: [] for e in ALLENG}
        self.dma_n = 0
        self.dma_tokens = []
        self.n_ops = 0

    def _need(self, eng, tok, waits):
        if tok is None:
            return
        key, val, teng = tok
        if self.seen[eng].get(key, 0) >= val:
            return
        self.seen[eng][key] = val
        waits.append((key, val))

    def _deps(self, eng, reads, writes):
        waits = []
        for b in reads:
            if b.excl:
                if b.w is not None and b.w[2] != eng:
                    self._need(eng, b.w, waits)
                continue
            self._need(eng, b.w, waits)
        for b in writes:
            if b.excl:
                if b.w is not None and b.w[2] != eng:
                    self._need(eng, b.w, waits)
                continue
            self._need(eng, b.w, waits)
            for t in b.r:
                if t[2] == eng and t[0] == eng:
                    continue
                self._need(eng, t, waits)
        waits += self.pending[eng]
        self.pending[eng] = []
        d = {}
        for k, v in waits:
            d[k] = max(d.get(k, 0), v)
        return list(d.items())

    def _commit(self, tok, reads, writes):
        for b in reads:
            if b.excl:
                b.w = tok
                continue
            b.r.append(tok)
        for b in writes:
            b.w = tok
            b.r = []

    def op(self, eng, fn, reads=(), writes=()):
        waits = self._deps(eng, reads, writes)
        self.cnt[eng] += 1
        tok = (eng, self.cnt[eng], eng)
        if eng == "pe":
            waits = [w for w in waits if w[0] != "pe"]
        self.ops[eng].append((waits, fn, (eng, 1)))
        self._commit(tok, reads, writes)
        self.n_ops += 1
        return tok

    def dma(self, out, in_, reads=(), writes=(), q="sp", **kw):
        waits = self._deps(q, reads, writes)
        n = self.dma_n
        self.dma_n += 1
        slot = n % N_DMA_SEMS
        gen = n // N_DMA_SEMS + 1
        key = ("dma", slot)
        if n >= N_DMA_SEMS:
            w2 = []
            self._need(q, self.dma_tokens[n - N_DMA_SEMS], w2)
            waits = waits + w2
        tok = (key, 16 * gen, q)
        self.dma_tokens.append(tok)

        def fn(e, out=out, in_=in_, kw=kw):
            return e.dma_start(out=out, in_=in_, **kw)
        self.ops[q].append((waits, fn, (key, 16)))
        self._commit(tok, reads, writes)
        self.n_ops += 1
        return tok

    def barrier(self):
        toks = [(e, self.cnt[e], e) for e in COMPUTE if self.cnt[e] > 0]
        n = self.dma_n
        toks += [self.dma_tokens[i] for i in range(max(0, n - N_DMA_SEMS), n)]
        for e in ALLENG:
            for t in toks:
                if t[0] == e:
                    continue
                w = []
                self._need(e, t, w)
                self.pending[e] += w

    def finalize(self, nc):
        with contextlib.ExitStack() as st:
            sems = {}
            for e in COMPUTE:
                sems[e] = st.enter_context(nc.semaphore("s_" + e))
            for i in range(min(N_DMA_SEMS, max(self.dma_n, 1))):
                sems[("dma", i)] = st.enter_context(nc.semaphore("d%d" % i))
            block = st.enter_context(nc.Block())
            fin = []
            n = self.dma_n
            for i in range(max(0, n - N_DMA_SEMS), n):
                self._need("sp", self.dma_tokens[i], fin)
            d = {}
            for k, v in fin:
                d[k] = max(d.get(k, 0), v)
            fin = list(d.items())

            def emit(h, name):
                for waits, fn, inc in self.ops[name]:
                    for k, v in waits:
                        h.wait_ge(sems[k], v)
                    fn(h).then_inc(sems[inc[0]], inc[1])
                if name == "sp":
                    for k, v in fin:
                        h.wait_ge(sems[k], v)

            @block.sync
            def _(e):
                emit(e, "sp")

            @block.tensor
            def _(e):
                emit(e, "pe")

            @block.scalar
            def _(e):
                emit(e, "act")

            @block.vector
            def _(e):
                emit(e, "dve")

            @block.gpsimd
            def _(e):
                emit(e, "pool")


def _bl(ts):
    out = []
    for t in ts:
        if isinstance(t, TL):
            out += t.bufs
        elif isinstance(t, Buf):
            out.append(t)
        else:
            out += list(t)
    return out


class K:
    def __init__(self, P):
        self.P = P

    def mm(self, out, lhsT, rhs, start=True, stop=True, r=(), w=()):
        self.P.op("pe", lambda e: e.matmul(out, lhsT=lhsT, rhs=rhs, start=start, stop=stop), _bl(r), _bl(w))

    def tr(self, out, in_, ident, r=(), w=()):
        self.P.op("pe", lambda e: e.transpose(out, in_, ident), _bl(r), _bl(w))

    def act(self, out, in_, func, bias=None, scale=None, accum=None, r=(), w=(), eng="act"):
        kw = {}
        if bias is not None:
            kw["bias"] = bias
        if scale is not None:
            kw["scale"] = scale
        if accum is not None:
            kw["accum_out"] = accum
        self.P.op(eng, lambda e: e.activation(out=out, in_=in_, func=func, **kw), _bl(r), _bl(w))

    def tt(self, out, in0, in1, op, r=(), w=(), eng="dve"):
        self.P.op(eng, lambda e: e.tensor_tensor(out=out, in0=in0, in1=in1, op=op), _bl(r), _bl(w))

    def ts(self, out, in0, s1, op0, s2=None, op1=None, r=(), w=(), eng="dve"):
        if op1 is None:
            self.P.op(eng, lambda e: e.tensor_scalar(out=out, in0=in0, scalar1=s1, scalar2=None, op0=op0),
                      _bl(r), _bl(w))
        else:
            self.P.op(eng, lambda e: e.tensor_scalar(out=out, in0=in0, scalar1=s1, scalar2=s2, op0=op0, op1=op1),
                      _bl(r), _bl(w))

    def stt(self, out, in0, scalar, in1, op0, op1, r=(), w=()):
        self.P.op("dve", lambda e: e.scalar_tensor_tensor(out=out, in0=in0, scalar=scalar, in1=in1, op0=op0, op1=op1),
                  _bl(r), _bl(w))

    def cp(self, out, in_, r=(), w=(), eng="dve"):
        if eng == "act":
            self.P.op("act", lambda e: e.copy(out=out, in_=in_), _bl(r), _bl(w))
        else:
            self.P.op(eng, lambda e: e.tensor_copy(out=out, in_=in_), _bl(r), _bl(w))

    def red(self, out, in_, op, r=(), w=(), axis=None):
        ax = AX.X if axis is None else axis
        self.P.op("dve", lambda e: e.tensor_reduce(out=out, in_=in_, axis=ax, op=op), _bl(r), _bl(w))

    def recip(self, out, in_, r=(), w=()):
        self.P.op("dve", lambda e: e.reciprocal(out=out, in_=in_), _bl(r), _bl(w))

    def memset(self, out, val, w=(), eng="pool"):
        self.P.op(eng, lambda e: e.memset(out, val), [], _bl(w))

    def dma(self, out, in_, r=(), w=(), q="sp"):
        self.P.dma(out, in_, _bl(r), _bl(w), q=q)


class Arena:
    def __init__(self, ap_f32, nwords):
        self.ap = ap_f32
        self.n = nwords
        self.top = 0

    def alloc(self, shape_free, dtype=F32, nsub=1):
        ne = int(np.prod(shape_free))
        words = ne if dtype in (F32, I32) else (ne + 1) // 2
        words = (words + 1) // 2 * 2
        off = self.top
        self.top += words
        assert self.top <= self.n, "SBUF arena overflow %d > %d" % (self.top, self.n)
        ap = self.ap[:, off:off + words]
        if dtype != F32:
            ap = ap.bitcast(dtype)
        ap = ap[:, 0:ne]
        if len(shape_free) == 2:
            ap = ap.rearrange("p (a b) -> p a b", a=shape_free[0])
        elif len(shape_free) == 3:
            ap = ap.rearrange("p (a b c) -> p a b c", a=shape_free[0], b=shape_free[1])
        return TL(ap, nsub)

    def mark(self):
        return self.top

    def release(self, m):
        self.top = m


class Cfg:
    def __init__(self, ntok, layers=(0, 1, 2, 3), final_norm=True, skip_ffn=False, skip_mixer=False):
        self.ntok = ntok
        self.layers = layers
        self.final_norm = final_norm
        self.skip_ffn = skip_ffn
        self.skip_mixer = skip_mixer
        self.stop = 99
        self.astop = 99
        self.gstop = 99


def build(cfg):
    NTOK = cfg.ntok
    NB = NTOK // 128
    nc = bass.Bass("TRN2", target_bir_lowering=False)

    def din(name, shape, dt=F32):
        return nc.dram_tensor(name, list(shape), dt, kind="ExternalInput").ap()

    x = din("x", [NTOK, D])
    mem = din("mem", [MEM_LEN, D])
    pos = din("pos", [NTOK], I32)
    ln_mix = din("ln_mix", [4, D])
    ln_ffn = din("ln_ffn", [4, D])
    ln_mem = din("ln_mem", [D])
    w_mem_kv = din("w_mem_kv", [4, D, 512])
    w_out = din("w_out", [4, D, D])
    w_gu = din("w_gate_up", [4, D, 2 * DFF])
    w_dn = din("w_down", [4, DFF, D])
    g_win = din("gdn_w_in", [2, D, GDN_IN])
    g_conv = din("gdn_conv", [2, 4, 2304])
    g_alog = din("gdn_A_log", [2, 6])
    g_dtb = din("gdn_dt_bias", [2, 6])
    g_norm = din("gdn_norm", [2, 128])
    s_wq = din("swa_w_q", [2, D, D])
    s_sink = din("swa_sinks", [2, 12])
    ln_kv = din("ln_kv", [D])
    w_kv = din("w_kv", [D, 256])
    ln_final = din("ln_final", [D])
    out = nc.dram_tensor("out", [NTOK, D], F32, kind="ExternalOutput").ap()
    S_in = din("s_in", [2, 128, 6, 128])
    hist_in = din("hist_in", [2, 128, 18, 4])
    kt_in = din("kt_in", [128, 128], BF16)
    vt_in = din("vt_in", [128, 128], BF16)
    hv_in = din("hv", [1])
    S_out = nc.dram_tensor("s_out", [2, 128, 6, 128], F32, kind="ExternalOutput").ap()
    hist_out = nc.dram_tensor("hist_out", [2, 128, 18, 4], F32, kind="ExternalOutput").ap()
    kt_out = nc.dram_tensor("kt_out", [128, 128], BF16, kind="ExternalOutput").ap()
    vt_out = nc.dram_tensor("vt_out", [128, 128], BF16, kind="ExternalOutput").ap()
    carry_buf = Buf()
    Rd = nc.dram_tensor("r_scratch", [NTOK, D], F32, kind="Internal").ap()

    P = Prog()
    k = K(P)
    st = contextlib.ExitStack()
    ARW = 52600
    arena_t = st.enter_context(nc.sbuf_tensor("arena", [128, ARW], F32))
    psum_t = st.enter_context(nc.psum_tensor("psum", [128, 8, 512], F32))
    A = Arena(arena_t, ARW)
    PS = [TL(psum_t[:, i, :], 4, excl=True) for i in range(8)]

    def psbf(i):
        return psum_t[:, i, :].bitcast(BF16)

    Rbuf = [Buf() for _ in range(NB)]
    Obuf = [Buf() for _ in range(NB)]

    dI = A.alloc((128,), I32)
    k_ = k
    P.op("pool", lambda e: e.iota(dI.ap, pattern=[[1, 128]], base=0, channel_multiplier=-1), [], dI.b())
    dF = A.alloc((128,), F32)
    k.cp(dF.ap, dI.ap, r=[dI], w=[dF])
    ident_f = A.alloc((128,), F32)
    ident_b = A.alloc((128,), BF16)
    k.ts(ident_f.ap, dF.ap, 0.0, ALU.is_equal, r=[dF], w=[ident_f])
    k.cp(ident_b.ap, ident_f.ap, r=[ident_f], w=[ident_b])
    maskBigL = A.alloc((128,), F32)
    k.ts(maskBigL.ap, dF.ap, 0.0, ALU.is_ge, BIG, ALU.mult, r=[dF], w=[maskBigL])
    maskNegT = A.alloc((128,), F32)
    k.ts(maskNegT.ap, dF.ap, 0.0, ALU.is_lt, -BIG, ALU.mult, r=[dF], w=[maskNegT])
    triu_f = A.alloc((128,), F32)
    k.ts(triu_f.ap, dF.ap, 0.0, ALU.is_ge, r=[dF], w=[triu_f])
    ones_f = A.alloc((128,), F32)
    k.memset(ones_f.ap, 1.0, w=[ones_f])
    ones_b = A.alloc((128,), BF16)
    k.memset(ones_b.ap, 1.0, w=[ones_b])
    maskS = A.alloc((256,), F32)
    k.ts(maskS.ap[:, 0:128], dF.ap, 0.0, ALU.is_le, -BIG, ALU.mult, r=[dF], w=[maskS])
    k.ts(maskS.ap[:, 128:256], dF.ap, 0.0, ALU.is_gt, -BIG, ALU.mult, r=[dF], w=[maskS])
    maskS0 = A.alloc((256,), F32)
    hvt = A.alloc((2,), F32)
    k.dma(hvt.ap[:, 0:1], hv_in.partition_broadcast(128), w=[hvt])
    k.ts(hvt.ap[:, 1:2], hvt.ap[:, 0:1], -1.0, ALU.add, BIG, ALU.mult, r=[hvt], w=[hvt])
    k.ts(maskS0.ap[:, 0:128], maskS.ap[:, 0:128], hvt.ap[:, 1:2], ALU.add, r=[maskS, hvt], w=[maskS0])
    k.cp(maskS0.ap[:, 128:256], maskS.ap[:, 128:256], r=[maskS], w=[maskS0], eng="pool")
    mhalf = A.alloc((8,), F32)
    k.memset(mhalf.ap, -0.5, w=[mhalf])
    epsb = A.alloc((2,), F32)
    k.memset(epsb.ap, EPS, w=[epsb])

    def bcast_load(src_1d, n):
        t = A.alloc((n,), F32)
        k.dma(t.ap, src_1d.partition_broadcast(128), w=[t])
        return t


    def rms_rstd(xt_ap, xt_r, junk, ss, rstd, n=D, eps=EPS):
        k.act(junk.ap, xt_ap, AF.Square, accum=ss.ap[:, 0:1], r=xt_r, w=[junk, ss])
        k.act(rstd.ap[:, 1:2], ss.ap[:, 0:1], AF.Identity, bias=epsb.ap[:, 0:1], scale=1.0 / n, r=[ss, epsb], w=[rstd])
        k.tt(rstd.ap[:, 0:1], rstd.ap[:, 1:2], mhalf.ap[:, 0:1], ALU.pow, r=[rstd, mhalf], w=[rstd], eng="pool")

    def norm_to_T(xt_ap, xt_r, gain, hT_ap, hT_w, scr, psb):
        junk, ss, rstd, hb = scr
        rms_rstd(xt_ap, xt_r, junk, ss, rstd)
        k.stt(hb.ap, xt_ap, rstd.ap[:, 0:1], gain.ap, ALU.mult, ALU.mult, r=list(xt_r) + [rstd, gain], w=[hb])
        pb = psbf(psb)
        for c in range(8):
            k.tr(pb[:, c * 128:(c + 1) * 128], hb.ap[:, c * 128:(c + 1) * 128], ident_b.ap,
                 r=[hb, ident_b], w=[PS[psb]])
        k.cp(hT_ap, pb.rearrange("p (c t) -> p c t", c=8), r=[PS[psb]], w=hT_w, eng="act")

    def load_w_bf16(dst, src_ap):
        k.dma(dst.ap, src_ap, w=[dst], q="pool")

    def ffn_phase(l, last):
        m0 = A.mark()
        TT = 256
        NBT = TT // 128
        wgu = A.alloc((KC, 2 * DFF), BF16)
        wdn = A.alloc((FC, D), BF16)
        for c in range(KC):
            k.dma(wgu.ap[:, c, :], w_gu[l, c * 128:(c + 1) * 128, :], w=[wgu], q="pool")
        for c in range(FC):
            k.dma(wdn.ap[:, c, :], w_dn[l, c * 128:(c + 1) * 128, :], w=[wdn], q="pool")
        g_ffn = bcast_load(ln_ffn[l, :], D)
        g_lnfinal = bcast_load(ln_final, D) if last else None
        Rt = [A.alloc((NBT, D), F32, nsub=NBT) for _ in range(2)]
        hT = [A.alloc((KC, TT), BF16) for _ in range(2)]
        actT = A.alloc((FC, TT), BF16, nsub=FC)
        sg = [A.alloc((TT,), F32) for _ in range(2)]
        scr = (A.alloc((D,), BF16), A.alloc((2,), F32), A.alloc((2,), F32), A.alloc((D,), BF16))
        scr2 = (scr[0], A.alloc((2,), F32), A.alloc((2,), F32), None) if last else None
        ntile = NTOK // TT

        def load(i):
            for j in range(NBT):
                blk = i * NBT + j
                k.dma(Rt[i % 2].ap[:, j, :], Rd[blk * 128:(blk + 1) * 128, :], r=[Rbuf[blk]], w=Rt[i % 2].b(j))
        load(0)
        for i in range(ntile):
            if i + 1 < ntile:
                load(i + 1)
            rt = Rt[i % 2]
            ht = hT[i % 2]
            for j in range(NBT):
                norm_to_T(rt.ap[:, j, :], rt.b(j), g_ffn, ht.ap[:, :, j * 128:(j + 1) * 128], [ht], scr, 0)
            for c in range(FC):
                pg = PS[1 + (c % 2) * 2]
                pu = PS[2 + (c % 2) * 2]
                for kk in range(KC):
                    k.mm(pg.ap[:, 0:TT], wgu.ap[:, kk, c * 128:(c + 1) * 128], ht.ap[:, kk, :],
                         start=(kk == 0), stop=(kk == KC - 1), r=[wgu, ht], w=[pg])
                for kk in range(KC):
                    k.mm(pu.ap[:, 0:TT], wgu.ap[:, kk, DFF + c * 128:DFF + (c + 1) * 128], ht.ap[:, kk, :],
                         start=(kk == 0), stop=(kk == KC - 1), r=[wgu, ht], w=[pu])
                s = sg[c % 2]
                k.act(s.ap, pg.ap[:, 0:TT], AF.Silu, r=[pg], w=[s])
                k.tt(actT.ap[:, c, :], s.ap, pu.ap[:, 0:TT], ALU.mult, r=[s, pu], w=actT.b(c))
            for j in range(NBT):
                blk = i * NBT + j
                for n in range(2):
                    py = PS[5 + n]
                    for c in range(FC):
                        k.mm(py.ap, actT.ap[:, c, j * 128:(j + 1) * 128], wdn.ap[:, c, n * 512:(n + 1) * 512],
                             start=(c == 0), stop=(c == FC - 1), r=[wdn] + actT.b(c), w=[py])
                    k.tt(rt.ap[:, j, n * 512:(n + 1) * 512], rt.ap[:, j, n * 512:(n + 1) * 512], py.ap, ALU.add,
                         r=[py] + rt.b(j), w=rt.b(j))
                if last and cfg.final_norm:
                    junk, ss, rstd, ob = scr2
                    rms_rstd(rt.ap[:, j, :], rt.b(j), junk, ss, rstd)
                    k.stt(rt.ap[:, j, :], rt.ap[:, j, :], rstd.ap[:, 0:1], g_lnfinal.ap, ALU.mult, ALU.mult,
                          r=rt.b(j) + [rstd, g_lnfinal], w=rt.b(j))
                    k.dma(out[blk * 128:(blk + 1) * 128, :], rt.ap[:, j, :], r=rt.b(j), w=[Obuf[blk]])
                elif last:
                    k.dma(out[blk * 128:(blk + 1) * 128, :], rt.ap[:, j, :], r=rt.b(j), w=[Obuf[blk]])
                else:
                    k.dma(Rd[blk * 128:(blk + 1) * 128, :], rt.ap[:, j, :], r=rt.b(j), w=[Rbuf[blk]])
        P.barrier()
        A.release(m0)

    memKT = A.alloc((4, 2, MEM_LEN), BF16)
    memV = A.alloc((4, 2, 256), BF16)
    has_b = any(l >= 2 for l in cfg.layers)
    if has_b:
        ropeC = A.alloc((NB, 8), F32)
        ropeS = A.alloc((NB, 8), F32)
        KTd = nc.dram_tensor("kt_scratch", [128, 128 + NTOK], BF16, kind="Internal").ap()
        Vtd = nc.dram_tensor("vt_scratch", [128, NB + 1, 128], BF16, kind="Internal").ap()
        KVbuf = Buf()

    def setup_phase():
        m0 = A.mark()
        g_lnmem = bcast_load(ln_mem, D)
        mt = A.alloc((2, D), F32)
        memT = A.alloc((KC, MEM_LEN), BF16)
        scr = (A.alloc((D,), BF16), A.alloc((2,), F32), A.alloc((2,), F32), A.alloc((D,), BF16))
        for j in range(2):
            k.dma(mt.ap[:, j, :], mem[j * 128:(j + 1) * 128, :], w=[mt])
        for j in range(2):
            norm_to_T(mt.ap[:, j, :], [mt], g_lnmem, memT.ap[:, :, j * 128:(j + 1) * 128], [memT], scr, 0)
        wm = [A.alloc((KC, 512), BF16) for _ in range(2)]
        for l in range(4):
            w = wm[l % 2]
            k.dma(w.ap, w_mem_kv[l].rearrange("(c p) n -> p c n", p=128), w=[w], q="pool")
            for c in range(2):
                ps = PS[1 + c]
                for kk in range(KC):
                    k.mm(ps.ap[:, 0:MEM_LEN], w.ap[:, kk, c * 128:(c + 1) * 128], memT.ap[:, kk, :],
                         start=(kk == 0), stop=(kk == KC - 1), r=[w, memT], w=[ps])
                k.cp(memKT.ap[:, l, c, :], ps.ap[:, 0:MEM_LEN], r=[ps], w=[memKT])
            for mb in range(2):
                ps = PS[3 + mb]
                for kk in range(KC):
                    k.mm(ps.ap[:, 0:256], memT.ap[:, kk, mb * 128:(mb + 1) * 128], w.ap[:, kk, 256:512],
                         start=(kk == 0), stop=(kk == KC - 1), r=[w, memT], w=[ps])
                k.cp(memV.ap[:, l, mb, :], ps.ap[:, 0:256], r=[ps], w=[memV], eng="act")
        if has_b:
            pi_ = A.alloc((128,), I32)
            pf_ = A.alloc((128,), F32)
            if NB < 128:
                k.memset(pf_.ap, 0.0, w=[pf_])
            k.dma(pi_.ap[0:NB, :], pos.rearrange("(b p) -> b p", p=128), w=[pi_])
            k.cp(pf_.ap[0:NB, :], pi_.ap[0:NB, :], r=[pi_], w=[pf_])
            pT = PS[5]
            k.tr(pT.ap[:, 0:128], pf_.ap, ident_f.ap, r=[pf_, ident_f], w=[pT])
            posf = A.alloc((NB,), F32)
            k.cp(posf.ap, pT.ap[:, 0:NB], r=[pT], w=[posf])
            ang = A.alloc((NB, 8), F32)
            inv = (np.float32(500000.0) ** (-(np.arange(0, 16, 2).astype(np.float32)) / np.float32(16))).astype(np.float32)
            for j in range(8):
                k.ts(ang.ap[:, :, j], posf.ap, float(inv[j]), ALU.mult, r=[posf], w=[ang])
            tq = A.alloc((NB, 8), F32)
            ki = A.alloc((NB, 8), I32)
            kf = A.alloc((NB, 8), F32)
            TWO_PI = 2.0 * np.pi
            C1 = 6.28125
            C2 = TWO_PI - C1
            k.ts(tq.ap, ang.ap, 1.0 / TWO_PI, ALU.mult, r=[ang], w=[tq])
            k.cp(ki.ap, tq.ap, r=[tq], w=[ki])
            k.cp(kf.ap, ki.ap, r=[ki], w=[kf])
            rr = A.alloc((NB, 8), F32)
            k.stt(rr.ap, kf.ap, -C1, ang.ap, ALU.mult, ALU.add, r=[kf, ang], w=[rr])
            k.stt(rr.ap, kf.ap, -C2, rr.ap, ALU.mult, ALU.add, r=[kf, rr], w=[rr])
            k.ts(tq.ap, rr.ap, float(np.pi), ALU.is_gt, -TWO_PI, ALU.mult, r=[rr], w=[tq])
            k.tt(rr.ap, rr.ap, tq.ap, ALU.add, r=[rr, tq], w=[rr])
            k.ts(tq.ap, rr.ap, float(-np.pi), ALU.is_lt, TWO_PI, ALU.mult, r=[rr], w=[tq])
            k.tt(rr.ap, rr.ap, tq.ap, ALU.add, r=[rr, tq], w=[rr])
            k.act(ropeS.ap, rr.ap, AF.Sin, r=[rr], w=[ropeS])
            k.act(tq.ap, rr.ap, AF.Sin, scale=0.5, r=[rr], w=[tq])
            k.tt(tq.ap, tq.ap, tq.ap, ALU.mult, r=[tq], w=[tq])
            k.ts(ropeC.ap, tq.ap, -2.0, ALU.mult, 1.0, ALU.add, r=[tq], w=[ropeC])
        P.barrier()
        A.release(m0)

    def rope(dst4, src4, b, H1, H2, tmp, r, w):
        c = ropeC.ap[:, b, :].unsqueeze(1).unsqueeze(1).to_broadcast([128, H1, H2, 8])
        s_ = ropeS.ap[:, b, :].unsqueeze(1).unsqueeze(1).to_broadcast([128, H1, H2, 8])
        x1 = src4[:, :, :, 0:8]
        x2 = src4[:, :, :, 8:16]
        t1 = tmp.ap[:, 0:H1 * H2 * 8].rearrange("p (a b d) -> p a b d", a=H1, b=H2)
        t2 = tmp.ap[:, H1 * H2 * 8:2 * H1 * H2 * 8].rearrange("p (a b d) -> p a b d", a=H1, b=H2)
        rr_ = list(r) + [ropeC, ropeS]
        k.tt(t1, x1, c, ALU.mult, r=rr_, w=[tmp])
        k.tt(t2, x2, s_, ALU.mult, r=rr_, w=[tmp])
        k.tt(dst4[:, :, :, 0:8], t1, t2, ALU.subtract, r=[tmp], w=w)
        k.tt(t1, x2, c, ALU.mult, r=rr_, w=[tmp])
        k.tt(t2, x1, s_, ALU.mult, r=rr_, w=[tmp])
        k.tt(dst4[:, :, :, 8:16], t1, t2, ALU.add, r=[tmp], w=w)
        k.cp(dst4[:, :, :, 16:64], src4[:, :, :, 16:64], r=r, w=w, eng="pool" if False else "dve")

    def kv_phase():
        m0 = A.mark()
        g = bcast_load(ln_kv, D)
        wkv = A.alloc((KC, 256), BF16)
        k.dma(wkv.ap, w_kv.rearrange("(c p) n -> p c n", p=128), w=[wkv], q="pool")
        KTp = A.alloc((128 + NTOK,), BF16)
        Vtp = A.alloc((NB + 1, 128), BF16)
        k.dma(KTp.ap[:, 0:128], kt_in, w=[KTp])
        k.dma(Vtp.ap[:, 0, :], vt_in, w=[Vtp])
        Rt = [A.alloc((D,), F32) for _ in range(2)]
        hT = [A.alloc((KC, 128), BF16) for _ in range(2)]
        scr = (A.alloc((D,), BF16), A.alloc((2,), F32), A.alloc((2,), F32), A.alloc((D,), BF16))
        kvf = A.alloc((256,), F32)
        kb = A.alloc((128,), BF16)
        tmp = A.alloc((2 * 2 * 8,), F32)
        k.dma(Rt[0].ap, Rd[0:128, :], r=[Rbuf[0]], w=[Rt[0]])
        for b in range(NB):
            if b + 1 < NB:
                k.dma(Rt[(b + 1) % 2].ap, Rd[(b + 1) * 128:(b + 2) * 128, :], r=[Rbuf[b + 1]], w=[Rt[(b + 1) % 2]])
            rt, ht = Rt[b % 2], hT[b % 2]
            norm_to_T(rt.ap, [rt], g, ht.ap, [ht], scr, 0)
            ps = PS[1 + b % 2]
            for kk in range(KC):
                k.mm(ps.ap[:, 0:256], ht.ap[:, kk, :], wkv.ap[:, kk, :], start=(kk == 0), stop=(kk == KC - 1),
                     r=[ht, wkv], w=[ps])
            k.cp(kvf.ap, ps.ap[:, 0:256], r=[ps], w=[kvf], eng="act")
            k.cp(Vtp.ap[:, b + 1, :], kvf.ap[:, 128:256], r=[kvf], w=[Vtp], eng="act")
            src4 = kvf.ap[:, 0:128].rearrange("p (a b d) -> p a b d", a=1, b=2)
            dst4 = kb.ap.rearrange("p (a b d) -> p a b d", a=1, b=2)
            rope(dst4, src4, b, 1, 2, tmp, [kvf], [kb])
            pb = psbf(3 + b % 2)
            k.tr(pb[:, 0:128], kb.ap, ident_b.ap, r=[kb, ident_b], w=[PS[3 + b % 2]])
            k.cp(KTp.ap[:, 128 + b * 128:128 + (b + 1) * 128], pb[:, 0:128], r=[PS[3 + b % 2]], w=[KTp], eng="act")
        k.dma(KTd, KTp.ap, r=[KTp], w=[KVbuf])
        k.dma(Vtd, Vtp.ap, r=[Vtp], w=[KVbuf])
        k.dma(kt_out, KTp.ap[:, NTOK:NTOK + 128], r=[KTp], w=[carry_buf])
        k.dma(vt_out, Vtp.ap[:, NB, :], r=[Vtp], w=[carry_buf])
        P.barrier()
        A.release(m0)

    def attn_group(sc_ps, nh, nk, mask_ap, sink_ap, scale, lhs_list, rhs_list, v_list, o_ps_ap, o_dst4, o_src4,
                   S, extra_r, o_w):
        Sm, mx, nm, sums, es, PTt, Pb = S
        nkb = nk // 128
        per_bank = 512 // nk
        for h in range(nh):
            ps = sc_ps[h // per_bank]
            o = (h % per_bank) * nk
            k.mm(ps.ap[:, o:o + nk], lhs_list[h], rhs_list[h], r=extra_r, w=[ps])
        if cfg.astop <= 1:
            return
        for bi, ps in enumerate(sc_ps):
            h0 = bi * per_bank
            src = ps.ap.rearrange("p (h n) -> p h n", h=per_bank)
            dst = Sm.ap[:, h0:h0 + per_bank, :]
            if mask_ap is not None:
                k.tt(dst, src, mask_ap.unsqueeze(1).to_broadcast([128, per_bank, nk]), ALU.add, r=[ps] + list(extra_r), w=[Sm])
            else:
                k.cp(dst, src, r=[ps], w=[Sm])
        if cfg.astop <= 2:
            return
        k.red(mx.ap[:, 0:nh], Sm.ap[:, 0:nh, :], ALU.max, r=[Sm], w=[mx])
        if cfg.astop <= 3:
            return
        if sink_ap is not None:
            k.stt(mx.ap[:, 0:nh], mx.ap[:, 0:nh], scale, sink_ap, ALU.mult, ALU.max, r=[mx] + list(extra_r), w=[mx])
            k.ts(nm.ap[:, 0:nh], mx.ap[:, 0:nh], -1.0, ALU.mult, r=[mx], w=[nm])
            k.tt(es.ap[:, 0:nh], sink_ap, mx.ap[:, 0:nh], ALU.subtract, r=[mx] + list(extra_r), w=[es])
            k.act(es.ap[:, 0:nh], es.ap[:, 0:nh], AF.Exp, r=[es], w=[es])
        else:
            k.ts(nm.ap[:, 0:nh], mx.ap[:, 0:nh], -scale, ALU.mult, r=[mx], w=[nm])
        if cfg.astop <= 4:
            return
        for h in range(nh):
            k.act(Pb.ap[:, h, :], Sm.ap[:, h, :], AF.Exp, bias=nm.ap[:, h:h + 1], scale=scale,
                  accum=sums.ap[:, h:h + 1], r=[Sm, nm], w=[Pb, sums])
        k.act(mx.ap[:, 0:nh], sums.ap[:, 0:nh], AF.Copy, r=[sums], w=[mx])
        if sink_ap is not None:
            k.tt(mx.ap[:, 0:nh], mx.ap[:, 0:nh], es.ap[:, 0:nh], ALU.add, r=[mx, es], w=[mx])
        k.recip(sums.ap[:, 0:nh], mx.ap[:, 0:nh], r=[mx], w=[sums])
        if cfg.astop <= 5:
            return
        pb = psbf(0)
        for h in range(nh):
            for kb_ in range(nkb):
                i = h * nkb + kb_
                k.tr(pb[:, i * 128:(i + 1) * 128], Pb.ap[:, h, kb_ * 128:(kb_ + 1) * 128], ident_b.ap,
                     r=[Pb, ident_b], w=[PS[0]])
        n_t = nh * nkb
        k.cp(PTt.ap[:, 0:n_t, :], pb[:, 0:n_t * 128].rearrange("p (c t) -> p c t", c=n_t), r=[PS[0]], w=[PTt], eng="act")
        if cfg.astop <= 6:
            return
        for h in range(nh):
            for kb_ in range(nkb):
                k.mm(o_ps_ap[:, h * 64:(h + 1) * 64], PTt.ap[:, h * nkb + kb_, :], v_list[h][kb_],
                     start=(kb_ == 0), stop=(kb_ == nkb - 1), r=[PTt] + list(extra_r), w=[PS[7]])
        if cfg.astop <= 7:
            return
        k.tt(o_dst4, o_src4, sums.ap[:, 0:nh].rearrange("p (a b) -> p a b", a=o_dst4.shape[1]).unsqueeze(3)
             .to_broadcast(list(o_dst4.shape)), ALU.mult, r=[PS[7], sums], w=o_w)

    def alloc_attn_scratch():
        return (A.alloc((4, 256), F32), A.alloc((4,), F32), A.alloc((4,), F32), A.alloc((4,), F32),
                A.alloc((4,), F32), A.alloc((8, 128), BF16), A.alloc((4, 256), BF16))

    def mem_attn(l, b_unused, mqT, mix_tok, S, o_half, scb=None):
        lhs = [mqT.ap[:, m % 2, m // 2, :] for m in range(4)]
        rhs = [memKT.ap[:, l, m // 2, :] for m in range(4)]
        vl = [[memV.ap[:, l, mb, m * 64:(m + 1) * 64] for mb in range(2)] for m in range(4)]
        o_ps = PS[7].ap[:, o_half * 256:(o_half + 1) * 256]
        o_src4 = o_ps.rearrange("p (a b d) -> p a b d", a=4, b=1)
        o_dst4 = mix_tok.ap[:, 768:1024].rearrange("p (a b d) -> p a b d", a=4, b=1)
        attn_group(scb or [PS[5], PS[6]], 4, 256, None, None, 0.125, lhs, rhs, vl, o_ps, o_dst4, o_src4, S,
                   [mqT, memKT, memV], [mix_tok])

    def out_proj_block(mix_tok, mixT, wo, rt, rt_b, blk, store=True):
        pb = psbf(0)
        for c in range(8):
            k.tr(pb[:, c * 128:(c + 1) * 128], mix_tok.ap[:, c * 128:(c + 1) * 128], ident_b.ap,
                 r=[mix_tok, ident_b], w=[PS[0]])
        k.cp(mixT.ap, pb.rearrange("p (c t) -> p c t", c=8), r=[PS[0]], w=[mixT], eng="act")
        for n in range(2):
            py = PS[1 + n]
            for c in range(KC):
                k.mm(py.ap, mixT.ap[:, c, :], wo.ap[:, c, n * 512:(n + 1) * 512], start=(c == 0), stop=(c == KC - 1),
                     r=[mixT, wo], w=[py])
            k.tt(rt[:, n * 512:(n + 1) * 512], rt[:, n * 512:(n + 1) * 512], py.ap, ALU.add, r=[py] + rt_b, w=rt_b)
        if store:
            k.dma(Rd[blk * 128:(blk + 1) * 128, :], rt, r=rt_b, w=[Rbuf[blk]])

    def swa_phase(l):
        bl = l - 2
        m0 = A.mark()
        g = bcast_load(ln_mix[l, :], D)
        wq = A.alloc((KC, D), BF16)
        wo = A.alloc((KC, D), BF16)
        k.dma(wq.ap, s_wq[bl].rearrange("(c p) n -> p c n", p=128), w=[wq], q="pool")
        k.dma(wo.ap, w_out[l].rearrange("(c p) n -> p c n", p=128), w=[wo], q="pool")
        KTp = A.alloc((128 + NTOK,), BF16)
        Vtp = A.alloc((NB + 1, 128), BF16)
        k.dma(KTp.ap, KTd, r=[KVbuf], w=[KTp])
        k.dma(Vtp.ap, Vtd, r=[KVbuf], w=[Vtp])
        skp = A.alloc((12,), F32)
        for g_ in range(2):
            P.dma(skp.ap.rearrange("p (j g) -> p j g", g=2)[:, :, g_], s_sink[bl, g_ * 6:(g_ + 1) * 6].partition_broadcast(128),
                  [], skp.b(), allow_slow_non_contiguous=True)
        Rt = [A.alloc((D,), F32) for _ in range(2)]
        hT = A.alloc((KC, 128), BF16)
        scr = (A.alloc((D,), BF16), A.alloc((2,), F32), A.alloc((2,), F32), A.alloc((D,), BF16))
        qf = A.alloc((D,), F32)
        qb = A.alloc((D,), BF16)
        qT = A.alloc((2, KC, 128), BF16)
        k.memset(qT.ap, 0.0, w=[qT])
        tmp = A.alloc((2 * 12 * 8,), F32)
        mix_tok = A.alloc((D,), BF16)
        mixT = A.alloc((KC, 128), BF16)
        S = alloc_attn_scratch()
        k.dma(Rt[0].ap, Rd[0:128, :], r=[Rbuf[0]], w=[Rt[0]])
        for b in range(NB):
            if b + 1 < NB:
                k.dma(Rt[(b + 1) % 2].ap, Rd[(b + 1) * 128:(b + 2) * 128, :], r=[Rbuf[b + 1]], w=[Rt[(b + 1) % 2]])
            rt = Rt[b % 2]
            norm_to_T(rt.ap, [rt], g, hT.ap, [hT], scr, 0)
            for n in range(2):
                ps = PS[1 + n]
                for kk in range(KC):
                    k.mm(ps.ap, hT.ap[:, kk, :], wq.ap[:, kk, n * 512:(n + 1) * 512], start=(kk == 0), stop=(kk == KC - 1),
                         r=[hT, wq], w=[ps])
                k.cp(qf.ap[:, n * 512:(n + 1) * 512], ps.ap, r=[ps], w=[qf], eng="act")
            if cfg.stop <= 1:
                continue
            src4 = qf.ap[:, 0:768].rearrange("p (g j d) -> p g j d", g=2, j=6)
            dst4 = qb.ap[:, 0:768].rearrange("p (j g d) -> p g j d", g=2, j=6)
            rope(dst4, src4, b, 2, 6, tmp, [qf], [qb])
            k.cp(qb.ap[:, 768:1024], qf.ap[:, 768:1024], r=[qf], w=[qb], eng="act")
            if cfg.stop <= 2:
                continue
            pb = psbf(0)
            for c in range(8):
                k.tr(pb[:, c * 128:(c + 1) * 128], qb.ap[:, c * 128:(c + 1) * 128], ident_b.ap, r=[qb, ident_b], w=[PS[0]])
            pbv = pb.rearrange("p (c t) -> p c t", c=8)
            k.cp(qT.ap[0:64, 0, :, :], pbv[0:64], r=[PS[0]], w=[qT], eng="act")
            k.cp(qT.ap[64:128, 1, :, :], pbv[64:128], r=[PS[0]], w=[qT], eng="dve")
            mask = maskS0 if b == 0 else maskS
            if cfg.stop <= 3:
                continue
            for gi in range(3):
                heads = [(2 * gi + jj, g_) for jj in range(2) for g_ in range(2)]
                lhs = [qT.ap[:, g_, j, :] for (j, g_) in heads]
                rhs = [KTp.ap[:, b * 128:(b + 2) * 128] for (j, g_) in heads]
                vl = [[Vtp.ap[:, b + kb_, g_ * 64:(g_ + 1) * 64] for kb_ in range(2)] for (j, g_) in heads]
                half = gi % 2
                o_ps = PS[7].ap[:, half * 256:(half + 1) * 256]
                o_src4 = o_ps.rearrange("p (j g d) -> p j g d", j=2, g=2)
                o_dst4 = mix_tok.ap[:, 0:768].rearrange("p (g j d) -> p j g d", g=2, j=6)[:, 2 * gi:2 * gi + 2, :, :]
                scb = [PS[3], PS[4]] if gi % 2 == 0 else [PS[5], PS[6]]
                attn_group(scb, 4, 256, mask.ap, skp.ap[:, 4 * gi:4 * gi + 4], 0.125, lhs, rhs, vl, o_ps, o_dst4, o_src4,
                           S, [qT, KTp, Vtp, mask, skp], [mix_tok])
            if cfg.stop <= 4:
                continue
            mq = TL(qT.ap[:, :, 6:8, :])
            mq.bufs = qT.bufs
            mem_attn(l, b, mq, mix_tok, S, 1)
            if cfg.stop <= 5:
                continue
            out_proj_block(mix_tok, mixT, wo, rt.ap, rt.b(), b)
        P.barrier()
        A.release(m0)

    def fin_phase():
        m0 = A.mark()
        g_lnfinal = bcast_load(ln_final, D)
        t = [A.alloc((D,), F32) for _ in range(2)]
        scr2 = (A.alloc((D,), BF16), A.alloc((2,), F32), A.alloc((2,), F32), A.alloc((D,), F32))
        for b in range(NB):
            tt_ = t[b % 2]
            k.dma(tt_.ap, Rd[b * 128:(b + 1) * 128, :], r=[Rbuf[b]], w=[tt_])
            if cfg.final_norm:
                junk, ss, rstd, ob = scr2
                rms_rstd(tt_.ap, [tt_], junk, ss, rstd)
                k.stt(ob.ap, tt_.ap, rstd.ap[:, 0:1], g_lnfinal.ap, ALU.mult, ALU.mult, r=[tt_, rstd, g_lnfinal], w=[ob])
                k.dma(out[b * 128:(b + 1) * 128, :], ob.ap, r=[ob], w=[Obuf[b]])
            else:
                k.dma(out[b * 128:(b + 1) * 128, :], tt_.ap, r=[tt_], w=[Obuf[b]])
        P.barrier()
        A.release(m0)

    def gdn_phase(l):
        a = l
        m0 = A.mark()
        TM = min(256, NTOK)
        NBM = TM // 128
        g = bcast_load(ln_mix[l, :], D)
        win = A.alloc((KC, GDN_IN), BF16)
        wo = A.alloc((KC, D), BF16)
        for c in range(KC):
            k.dma(win.ap[:, c, :], g_win[a, c * 128:(c + 1) * 128, :], w=[win], q="pool")
        k.dma(wo.ap, w_out[l].rearrange("(c p) n -> p c n", p=128), w=[wo], q="pool")
        cwr = A.alloc((4, 128), F32)
        k.dma(cwr.ap[0:18], g_conv[a].rearrange("j (c p) -> c j p", p=128), w=[cwr])
        cw = A.alloc((4, 18), F32)
        for j in range(4):
            k.tr(PS[3].ap[:, j * 32:j * 32 + 18], cwr.ap[0:18, j, :], ident_f.ap[0:18, 0:18], r=[cwr, ident_f], w=[PS[3]])
        k.cp(cw.ap, PS[3].ap[:, 0:128].rearrange("p (j c) -> p j c", j=4)[:, :, 0:18], r=[PS[3]], w=[cw])
        alog = bcast_load(g_alog[a, :], 6)
        dtb = bcast_load(g_dtb[a, :], 6)
        negA = A.alloc((6,), F32)
        k.act(negA.ap, alog.ap, AF.Exp, r=[alog], w=[negA])
        k.ts(negA.ap, negA.ap, -1.0, ALU.mult, r=[negA], w=[negA])
        ngb = bcast_load(g_norm[a, :], 128)
        k.ts(ngb.ap, ngb.ap, 0.5, ALU.mult, r=[ngb], w=[ngb])
        seli = A.alloc((6, 128), I32)
        sel = A.alloc((6, 128), F32)
        P.op("pool", lambda e: e.iota(seli.ap[0:6], pattern=[[1, 6], [0, 128]], base=0, channel_multiplier=-1), [], seli.b())
        k.ts(sel.ap[0:6], seli.ap[0:6], 0.0, ALU.is_equal, r=[seli], w=[sel])
        if cfg.gstop <= 1:
            P.barrier(); A.release(m0); return
        Rt = [A.alloc((D,), F32) for _ in range(2)]
        Ro = [A.alloc((D,), F32) for _ in range(2)]
        hT = A.alloc((KC, TM), BF16)
        scr = (A.alloc((D,), BF16), A.alloc((2,), F32), A.alloc((2,), F32), A.alloc((D,), BF16))
        cbuf = [A.alloc((TM + 4,), F32) for _ in range(2)]
        hist = A.alloc((18, 4), F32)
        k.dma(hist.ap, hist_in[a], w=[hist])
        acc = [A.alloc((TM,), F32) for _ in range(2)]
        tcv = [A.alloc((TM,), F32) for _ in range(2)]
        th = [A.alloc((TM,), F32) for _ in range(2)]
        sqb = [A.alloc((TM,), BF16) for _ in range(2)]
        rr = [A.alloc((TM,), F32) for _ in range(2)]
        qkvT = A.alloc((18, TM), BF16, nsub=18)
        mqT = A.alloc((2, 2, TM), BF16)
        k.memset(mqT.ap, 0.0, w=[mqT])
        ztok = A.alloc((NBM, 780), F32, nsub=NBM)
        gsc = A.alloc((10, NBM * 6), F32)
        gt = A.alloc((NBM, 6), F32)
        bt = A.alloc((NBM, 6), F32)
        Sf = A.alloc((6, 128), F32, nsub=6)
        Sb = A.alloc((6, 128), BF16, nsub=6)
        k.dma(Sf.ap, S_in[a], w=[Sf])
        k.cp(Sb.ap, Sf.ap, r=[Sf], w=[Sb])
        gcs = A.alloc((6,), F32)
        gcT = A.alloc((128,), F32)
        eg = A.alloc((6,), F32)
        ekl = A.alloc((6,), F32)
        gl = A.alloc((6,), F32)
        beg = A.alloc((6,), F32)
        bh = A.alloc((6,), F32)
        gtmp = A.alloc((6,), F32)
        HS = []
        for _ in range(2):
            d = {}
            for nm_ in ("Dm0", "DmL", "DmT", "decL", "decT", "ebc", "L", "LT", "M0", "M1", "MT0", "MT1", "TT"):
                d[nm_] = A.alloc((128,), F32)
            for nm_ in ("TTb", "AT", "QgT", "Kbg", "Kg", "Vb", "nWT", "Vn"):
                d[nm_] = A.alloc((128,), BF16)
            HS.append(d)
        Ot = A.alloc((6, 128), F32, nsub=6)
        ssO = A.alloc((6,), F32)
        rsO = A.alloc((6,), F32)
        ojunk = A.alloc((128,), BF16)
        tz = A.alloc((768,), F32)
        gz = A.alloc((768,), F32)
        mix_tok = A.alloc((D,), BF16)
        mixT = A.alloc((KC, 128), BF16)
        S = alloc_attn_scratch()
        nmac = NTOK // TM
        Q = lambda i, q_: PS[i].b(q_)

        def psq(i, q_):
            return PS[i].ap[:, q_ * 128:(q_ + 1) * 128]

        def psqb(i, q_):
            return PS[i].ap[:, q_ * 128:q_ * 128 + 64].bitcast(BF16)

        k.dma(Rt[0].ap, Rd[0:128, :], r=[Rbuf[0]], w=[Rt[0]])
        for mi in range(nmac):
            for j in range(NBM):
                blk = mi * NBM + j
                if blk + 1 < NB:
                    k.dma(Rt[(blk + 1) % 2].ap, Rd[(blk + 1) * 128:(blk + 2) * 128, :], r=[Rbuf[blk + 1]], w=[Rt[(blk + 1) % 2]])
                rt = Rt[blk % 2]
                norm_to_T(rt.ap, [rt], g, hT.ap[:, :, j * 128:(j + 1) * 128], [hT], scr, 0)
            if cfg.gstop <= 2:
                continue
            for c in range(20):
                ps = PS[1 + c % 2]
                col = c * 128 if c < 18 else 3084 + (c - 18) * 128
                for kk in range(KC):
                    k.mm(ps.ap[:, 0:TM], win.ap[:, kk, col:col + 128], hT.ap[:, kk, :], start=(kk == 0), stop=(kk == KC - 1),
                         r=[win, hT], w=[ps])
                if c >= 18:
                    cc = c - 18
                    k.cp(mqT.ap[0:64, 0, cc, :], ps.ap[0:64, 0:TM], r=[ps], w=[mqT], eng="act")
                    k.cp(mqT.ap[64:128, 1, cc, :], ps.ap[64:128, 0:TM], r=[ps], w=[mqT], eng="dve")
                    continue
                cb = cbuf[c % 2]
                k.cp(cb.ap[:, 0:3], hist.ap[:, c, 0:3], r=[hist], w=[cb], eng="pool")
                k.cp(cb.ap[:, 3:3 + TM], ps.ap[:, 0:TM], r=[ps], w=[cb], eng="act")
                k.cp(hist.ap[:, c, 0:3], cb.ap[:, TM:TM + 3], r=[cb], w=[hist], eng="pool")
                ac, tc = acc[c % 2], tcv[c % 2]
                k.ts(ac.ap, cb.ap[:, 0:TM], cw.ap[:, 0, c:c + 1], ALU.mult, r=[cb, cw], w=[ac], eng="pool")
                for tp in range(1, 4):
                    k.ts(tc.ap, cb.ap[:, tp:tp + TM], cw.ap[:, tp, c:c + 1], ALU.mult, r=[cb, cw], w=[tc], eng="pool")
                    k.tt(ac.ap, ac.ap, tc.ap, ALU.add, r=[ac, tc], w=[ac], eng="pool")
                t_ = th[c % 2]
                k.act(t_.ap, ac.ap, AF.Tanh, scale=0.5, r=[ac], w=[t_])
                if c >= 12:
                    k.stt(qkvT.ap[:, c, :], t_.ap, 1.0, ac.ap, ALU.add, ALU.mult, r=[t_, ac], w=qkvT.b(c))
                    continue
                k.stt(t_.ap, t_.ap, 1.0, ac.ap, ALU.add, ALU.mult, r=[t_, ac], w=[t_])
                sq = sqb[c % 2]
                k.tt(sq.ap, t_.ap, t_.ap, ALU.mult, r=[t_], w=[sq])
                pss = PS[3]
                k.mm(pss.ap[:, 0:TM], ones_b.ap, sq.ap, r=[ones_b, sq], w=[pss])
                r_ = rr[c % 2]
                k.ts(r_.ap, pss.ap[:, 0:TM], 0.25, ALU.mult, EPS, ALU.add, r=[pss], w=[r_])
                k.tt(r_.ap, r_.ap, mhalf.ap[:, 0:1].to_broadcast([128, TM]), ALU.pow, r=[r_, mhalf], w=[r_], eng="pool")
                kap = 0.5 * (128.0 ** -0.5) if c < 6 else 0.5
                k.stt(qkvT.ap[:, c, :], t_.ap, kap, r_.ap, ALU.mult, ALU.mult, r=[t_, r_], w=qkvT.b(c))
            if cfg.gstop <= 3:
                continue
            for j in range(NBM):
                for n in range(2):
                    ps = PS[1 + n]
                    wdt = 512 if n == 0 else 268
                    for kk in range(KC):
                        k.mm(ps.ap[:, 0:wdt], hT.ap[:, kk, j * 128:(j + 1) * 128], win.ap[:, kk, 2304 + n * 512:2304 + n * 512 + wdt],
                             start=(kk == 0), stop=(kk == KC - 1), r=[win, hT], w=[ps])
                    k.cp(ztok.ap[:, j, n * 512:n * 512 + wdt], ps.ap[:, 0:wdt], r=[ps], w=ztok.b(j), eng="act")
            if cfg.gstop <= 4:
                continue
            def G_(i):
                return gsc.ap[:, i, :].rearrange("p (j h) -> p j h", j=NBM)
            zl = ztok.ap[:, :, 768:774]
            al = ztok.ap[:, :, 774:780]
            dtb3 = dtb.ap.unsqueeze(1).to_broadcast([128, NBM, 6])
            negA3 = negA.ap.unsqueeze(1).to_broadcast([128, NBM, 6])
            k.tt(G_(0), al, dtb3, ALU.add, r=[ztok, dtb], w=[gsc])
            k.ts(G_(1), G_(0), -1.0, ALU.mult, r=[gsc], w=[gsc])
            k.tt(G_(1), G_(1), G_(0), ALU.max, r=[gsc], w=[gsc])
            k.act(G_(2), G_(1), AF.Exp, scale=-1.0, r=[gsc], w=[gsc])
            k.ts(G_(2), G_(2), 1.0, ALU.add, r=[gsc], w=[gsc])
            k.act(G_(3), G_(2), AF.Ln, r=[gsc], w=[gsc])
            k.stt(G_(4), G_(0), 0.0, G_(3), ALU.max, ALU.add, r=[gsc], w=[gsc])
            k.tt(gt.ap, G_(4), negA3, ALU.mult, r=[gsc, negA], w=[gt])
            k.act(G_(5), zl, AF.Exp, scale=-1.0, r=[ztok], w=[gsc])
            k.ts(G_(5), G_(5), 1.0, ALU.add, r=[gsc], w=[gsc])
            k.recip(bt.ap, G_(5), r=[gsc], w=[bt])
            if cfg.gstop <= 5:
                continue
            for j in range(NBM):
                blk = mi * NBM + j
                cs_ = slice(j * 128, (j + 1) * 128)
                k.dma(Ro[blk % 2].ap, Rd[blk * 128:(blk + 1) * 128, :], r=[Rbuf[blk]], w=[Ro[blk % 2]])
                pg = PS[7]
                k.mm(pg.ap[:, 0:6], triu_f.ap, gt.ap[:, j, :], r=[triu_f, gt], w=Q(7, 0))
                k.mm(pg.ap[:, 8:14], ones_f.ap, gt.ap[:, j, :], r=[ones_f, gt], w=Q(7, 0))
                if cfg.gstop <= 5.1:
                    continue
                k.cp(gcs.ap, pg.ap[:, 0:6], r=Q(7, 0), w=[gcs])
                if cfg.gstop <= 5.2:
                    continue
                k.act(eg.ap, pg.ap[:, 0:6], AF.Exp, r=Q(7, 0), w=[eg])
                k.act(gl.ap, pg.ap[:, 8:14], AF.Exp, r=Q(7, 0), w=[gl])
                if cfg.gstop <= 5.3:
                    continue
                k.tt(gtmp.ap, pg.ap[:, 8:14], gcs.ap, ALU.subtract, r=Q(7, 0) + [gcs], w=[gtmp])
                k.act(ekl.ap, gtmp.ap, AF.Exp, r=[gtmp], w=[ekl])
                k.tt(beg.ap, bt.ap[:, j, :], eg.ap, ALU.mult, r=[bt, eg], w=[beg])
                k.ts(bh.ap, bt.ap[:, j, :], 0.5, ALU.mult, r=[bt], w=[bh])
                if cfg.gstop <= 5.4:
                    continue
                k.tr(PS[7].ap[0:6, 128:256], gcs.ap, ident_f.ap, r=[gcs, ident_f], w=Q(7, 1))
                k.cp(gcT.ap[0:6, :], PS[7].ap[0:6, 128:256], r=Q(7, 1), w=[gcT])
                if cfg.gstop <= 6:
                    continue
                for h in range(6):
                    H = HS[h % 2]
                    kT = qkvT.ap[:, 6 + h, cs_]
                    qT_ = qkvT.ap[:, h, cs_]
                    vT_ = qkvT.ap[:, 12 + h, cs_]
                    if cfg.gstop <= 7:
                        break
                    k.mm(psq(3, 0), sel.ap[0:6, h, :], gcT.ap[0:6, :], r=[sel, gcT], w=Q(3, 0))
                    k.ts(H["Dm0"].ap, psq(3, 0), gcs.ap[:, h:h + 1], ALU.subtract, r=Q(3, 0) + [gcs], w=[H["Dm0"]])
                    k.act(H["ebc"].ap, psq(3, 0), AF.Exp, r=Q(3, 0), w=[H["ebc"]])
                    k.tt(H["DmL"].ap, H["Dm0"].ap, maskBigL.ap, ALU.add, r=[H["Dm0"], maskBigL], w=[H["DmL"]], eng="pool")
                    k.tt(H["DmT"].ap, H["Dm0"].ap, maskNegT.ap, ALU.add, r=[H["Dm0"], maskNegT], w=[H["DmT"]], eng="pool")
                    k.act(H["decL"].ap, H["DmL"].ap, AF.Exp, scale=-1.0, r=[H["DmL"]], w=[H["decL"]])
                    k.act(H["decT"].ap, H["DmT"].ap, AF.Exp, r=[H["DmT"]], w=[H["decT"]])
                    if cfg.gstop <= 8:
                        continue
                    k.mm(psq(3, 1), kT, kT, r=qkvT.b(6 + h), w=Q(3, 1))
                    k.mm(psq(3, 2), kT, qT_, r=qkvT.b(6 + h) + qkvT.b(h), w=Q(3, 2))
                    k.stt(H["L"].ap, psq(3, 1), bt.ap[:, j, h:h + 1], H["decL"].ap, ALU.mult, ALU.mult,
                          r=Q(3, 1) + [bt, H["decL"]], w=[H["L"]])
                    k.tt(H["AT"].ap, psq(3, 2), H["decT"].ap, ALU.mult, r=Q(3, 2) + [H["decT"]], w=[H["AT"]])
                    k.tt(H["QgT"].ap, qT_, H["ebc"].ap, ALU.mult, r=qkvT.b(h) + [H["ebc"]], w=[H["QgT"]])
                    k.tr(psq(3, 3), H["L"].ap, ident_f.ap, r=[H["L"], ident_f], w=Q(3, 3))
                    k.cp(H["LT"].ap, psq(3, 3), r=Q(3, 3), w=[H["LT"]], eng="act")
                    k.stt(H["TT"].ap, psq(3, 3), -1.0, ident_f.ap, ALU.mult, ALU.add, r=Q(3, 3) + [ident_f], w=[H["TT"]])
                    if cfg.gstop <= 9:
                        continue
                    M, MT = H["L"], H["LT"]
                    for lv in range(6):
                        Mn = H["M%d" % (lv % 2)]
                        MTn = H["MT%d" % (lv % 2)]
                        k.mm(psq(4, 0), MT.ap, M.ap, r=[MT, M], w=Q(4, 0))
                        if lv < 5:
                            k.mm(psq(4, 1), M.ap, MT.ap, r=[MT, M], w=Q(4, 1))
                        k.cp(Mn.ap, psq(4, 0), r=Q(4, 0), w=[Mn], eng="act")
                        if lv < 5:
                            k.cp(MTn.ap, psq(4, 1), r=Q(4, 1), w=[MTn], eng="dve")
                        k.mm(psq(4, 2), Mn.ap, H["TT"].ap, r=[Mn, H["TT"]], w=Q(4, 2))
                        k.tt(H["TT"].ap, H["TT"].ap, psq(4, 2), ALU.add, r=Q(4, 2) + [H["TT"]], w=[H["TT"]])
                        M, MT = Mn, MTn
                    k.cp(H["TTb"].ap, H["TT"].ap, r=[H["TT"]], w=[H["TTb"]], eng="pool")
                    if cfg.gstop <= 10:
                        continue
                    k.tr(psqb(5, 0), kT, ident_b.ap, r=qkvT.b(6 + h) + [ident_b], w=Q(5, 0))
                    k.tr(psqb(5, 1), vT_, ident_b.ap, r=qkvT.b(12 + h) + [ident_b], w=Q(5, 1))
                    k.ts(H["Kbg"].ap, psqb(5, 0), beg.ap[:, h:h + 1], ALU.mult, r=Q(5, 0) + [beg], w=[H["Kbg"]])
                    k.act(H["Kg"].ap, psqb(5, 0), AF.Copy, scale=ekl.ap[:, h:h + 1], r=Q(5, 0) + [ekl], w=[H["Kg"]])
                    k.ts(H["Vb"].ap, psqb(5, 1), bh.ap[:, h:h + 1], ALU.mult, r=Q(5, 1) + [bh], w=[H["Vb"]])
                    k.mm(psq(5, 2), H["Kbg"].ap, H["TTb"].ap, r=[H["Kbg"], H["TTb"]], w=Q(5, 2))
                    k.act(H["nWT"].ap, psq(5, 2), AF.Copy, scale=-1.0, r=Q(5, 2), w=[H["nWT"]])
                    if cfg.gstop <= 11:
                        continue
                    k.mm(psq(5, 3), H["TTb"].ap, H["Vb"].ap, start=True, stop=False, r=[H["TTb"], H["Vb"]], w=Q(5, 3))
                    k.mm(psq(5, 3), H["nWT"].ap, Sb.ap[:, h, :], start=False, stop=True, r=[H["nWT"]] + Sb.b(h), w=Q(5, 3))
                    k.cp(H["Vn"].ap, psq(5, 3), r=Q(5, 3), w=[H["Vn"]], eng="act")
                    if cfg.gstop <= 12:
                        continue
                    oq = (6, h % 4) if h < 4 else (4, 3) if h == 4 else (7, 3)
                    k.mm(psq(*oq), H["QgT"].ap, Sb.ap[:, h, :], start=True, stop=False, r=[H["QgT"]] + Sb.b(h), w=Q(*oq))
                    k.mm(psq(*oq), H["AT"].ap, H["Vn"].ap, start=False, stop=True, r=[H["AT"], H["Vn"]], w=Q(*oq))
                    k.act(ojunk.ap, psq(*oq), AF.Square, accum=ssO.ap[:, h:h + 1], r=Q(*oq), w=[ojunk, ssO])
                    k.cp(Ot.ap[:, h, :], psq(*oq), r=Q(*oq), w=Ot.b(h))
                    k.mm(psq(7, 2), H["Kg"].ap, H["Vn"].ap, r=[H["Kg"], H["Vn"]], w=Q(7, 2))
                    k.stt(Sf.ap[:, h, :], Sf.ap[:, h, :], gl.ap[:, h:h + 1], psq(7, 2), ALU.mult, ALU.add,
                          r=Q(7, 2) + [gl] + Sf.b(h), w=Sf.b(h))
                    k.cp(Sb.ap[:, h, :], Sf.ap[:, h, :], r=Sf.b(h), w=Sb.b(h), eng="act")
                if cfg.gstop <= 13:
                    continue
                k.act(rsO.ap, ssO.ap, AF.Identity, bias=epsb.ap[:, 0:1], scale=1.0 / 128, r=[ssO, epsb], w=[rsO])
                k.tt(rsO.ap, rsO.ap, mhalf.ap[:, 0:6], ALU.pow, r=[rsO, mhalf], w=[rsO], eng="pool")
                zz = ztok.ap[:, j, 0:768]
                k.act(tz.ap, zz, AF.Tanh, scale=0.5, r=ztok.b(j), w=[tz])
                k.stt(tz.ap, tz.ap, 1.0, zz, ALU.add, ALU.mult, r=[tz] + ztok.b(j), w=[tz])
                tz3 = tz.ap.rearrange("p (h d) -> p h d", h=6)
                gz3 = gz.ap.rearrange("p (h d) -> p h d", h=6)
                k.tt(gz3, tz3, ngb.ap.unsqueeze(1).to_broadcast([128, 6, 128]), ALU.mult, r=[tz, ngb], w=[gz])
                k.tt(gz3, gz3, rsO.ap.unsqueeze(2).to_broadcast([128, 6, 128]), ALU.mult, r=[gz, rsO], w=[gz])
                k.tt(mix_tok.ap[:, 0:768].rearrange("p (h d) -> p h d", h=6), Ot.ap, gz3, ALU.mult, r=[Ot, gz], w=[mix_tok])
                if cfg.gstop <= 14:
                    continue
                mq = TL(mqT.ap[:, :, :, cs_])
                mq.bufs = mqT.bufs
                mem_attn(l, blk, mq, mix_tok, S, 0, [PS[1], PS[2]])
                ro = Ro[blk % 2]
                out_proj_block(mix_tok, mixT, wo, ro.ap, ro.b(), blk)
        k.dma(S_out[a], Sf.ap, r=[Sf], w=[carry_buf])
        k.dma(hist_out[a], hist.ap, r=[hist], w=[carry_buf])
        P.barrier()
        A.release(m0)

    def init_phase():
        m0 = A.mark()
        t = [A.alloc((D,), F32) for _ in range(2)]
        for b in range(NB):
            k.dma(t[b % 2].ap, x[b * 128:(b + 1) * 128, :], w=[t[b % 2]])
            k.dma(Rd[b * 128:(b + 1) * 128, :], t[b % 2].ap, r=[t[b % 2]], w=[Rbuf[b]])
        P.barrier()
        A.release(m0)

    setup_phase()
    init_phase()
    nl = len(cfg.layers)
    for li, l in enumerate(cfg.layers):
        if l >= 2 and (li == 0 or cfg.layers[li - 1] < 2):
            kv_phase()
        if l < 2:
            gdn_phase(l)
        elif not cfg.skip_mixer:
            swa_phase(l)
        if not cfg.skip_ffn:
            ffn_phase(l, li == nl - 1)
    if cfg.skip_ffn:
        fin_phase()

    P.finalize(nc)
    st.close()
    return nc, P


INPUT_NAMES = ["ln_mix", "ln_ffn", "ln_mem", "w_mem_kv", "w_out", "w_gate_up", "w_down", "gdn_w_in", "gdn_conv",
               "gdn_A_log", "gdn_dt_bias", "gdn_norm", "swa_w_q", "swa_sinks", "ln_kv", "w_kv", "ln_final"]


import ml_dtypes

SEG = 4096


def _zeros_carry():
    return {"s_in": np.zeros((2, 128, 6, 128), np.float32), "hist_in": np.zeros((2, 128, 18, 4), np.float32),
            "kt_in": np.zeros((128, 128), ml_dtypes.bfloat16), "vt_in": np.zeros((128, 128), ml_dtypes.bfloat16),
            "hv": np.zeros((1,), np.float32)}


def run(cfg, inputs, ncores_batch, nseg=1):
    nc, P = build(cfg)
    shared = {n: np.ascontiguousarray(np.asarray(inputs[n], dtype=np.float32)) for n in INPUT_NAMES}
    carry = [_zeros_carry() for _ in ncores_batch]
    outs = [[] for _ in ncores_batch]
    for sgi in range(nseg):
        in_maps = []
        for ci, (b, t0) in enumerate(ncores_batch):
            ts = t0 + sgi * cfg.ntok
            m = dict(shared)
            m["x"] = np.ascontiguousarray(np.asarray(inputs["x"])[b, ts:ts + cfg.ntok, :], dtype=np.float32)
            m["mem"] = np.ascontiguousarray(np.asarray(inputs["mem"])[b], dtype=np.float32)
            m["pos"] = np.ascontiguousarray(np.asarray(inputs["positions"])[b, ts:ts + cfg.ntok], dtype=np.int32)
            m.update(carry[ci])
            in_maps.append(m)
        res = run_bass_kernel_spmd(nc, in_maps, core_ids=list(range(len(in_maps))))
        for ci, r in enumerate(res.results):
            outs[ci].append(r["out"])
            carry[ci] = {"s_in": np.asarray(r["s_out"]), "hist_in": np.asarray(r["hist_out"]),
                         "kt_in": np.asarray(r["kt_out"]), "vt_in": np.asarray(r["vt_out"]),
                         "hv": np.ones((1,), np.float32)}
    return [np.concatenate(o, axis=0) for o in outs]


def kernel(**inputs):
    B, S, _ = inputs["x"].shape
    cfg = Cfg(SEG)
    outs = run(cfg, inputs, [(b, 0) for b in range(B)], nseg=S // SEG)
    return np.stack(outs, axis=0).astype(np.float32)
```

```python
import contextlib
import numpy as np
import concourse.bass as bass
import concourse.mybir as mybir
from concourse.bass_utils import run_bass_kernel_spmd

F32 = mybir.dt.float32
BF16 = mybir.dt.bfloat16
I32 = mybir.dt.int32
AF = mybir.ActivationFunctionType
ALU = mybir.AluOpType
AX = mybir.AxisListType

D = 1024
KC = 8
DFF = 2816
FC = 22
GDN_IN = 3340
EPS = 1e-6
BIG = 1.0e30
MEM_LEN = 256

COMPUTE = ("pe", "act", "dve", "pool")
ALLENG = COMPUTE + ("sp",)
N_DMA_SEMS = 64


class Buf:
    __slots__ = ("w", "r", "excl")

    def __init__(self, excl=False):
        self.w = None
        self.r = []
        self.excl = excl


class TL:
    def __init__(self, ap, nsub=1, excl=False):
        self.ap = ap
        if excl:
            b = Buf(True)
            self.bufs = [b] * nsub
        else:
            self.bufs = [Buf() for _ in range(nsub)]

    def b(self, i=None):
        if i is None:
            return list(self.bufs)
        return [self.bufs[i]]


class Prog:
    def __init__(self):
        self.ops = {e: [] for e in ALLENG}
        self.cnt = {e: 0 for e in COMPUTE}
        self.seen = {e: {} for e in ALLENG}
        self.pending = {e: [] for e in ALLENG}
        self.dma_n = 0
        self.dma_tokens = []
        self.n_ops = 0

    def _need(self, eng, tok, waits):
        if tok is None:
            return
        key, val, teng = tok
        if self.seen[eng].get(key, 0) >= val:
            return
        self.seen[eng][key] = val
        waits.append((key, val))

    def _deps(self, eng, reads, writes):
        waits = []
        for b in reads:
            if b.excl:
                if b.w is not None and b.w[2] != eng:
                    self._need(eng, b.w, waits)
                continue
            self._need(eng, b.w, waits)
        for b in writes:
            if b.excl:
                if b.w is not None and b.w[2] != eng:
                    self._need(eng, b.w, waits)
                continue
            self._need(eng, b.w, waits)
            for t in b.r:
                if t[2] == eng and t[0] == eng:
                    continue
                self._need(eng, t, waits)
        waits += self.pending[eng]
        self.pending[eng] = []
        d = {}
        for k, v in waits:
            d[k] = max(d.get(k, 0), v)
        return list(d.items())

    def _commit(self, tok, reads, writes):
        for b in reads:
            if b.excl:
                b.w = tok
                continue
            b.r.append(tok)
        for b in writes:
            b.w = tok
            b.r = []

    def op(self, eng, fn, reads=(), writes=()):
        waits = self._deps(eng, reads, writes)
        self.cnt[eng] += 1
        tok = (eng, self.cnt[eng], eng)
        if eng == "pe":
            waits = [w for w in waits if w[0] != "pe"]
        self.ops[eng].append((waits, fn, (eng, 1)))
        self._commit(tok, reads, writes)
        self.n_ops += 1
        return tok

    def dma(self, out, in_, reads=(), writes=(), q="sp", **kw):
        waits = self._deps(q, reads, writes)
        n = self.dma_n
        self.dma_n += 1
        slot = n % N_DMA_SEMS
        gen = n // N_DMA_SEMS + 1
        key = ("dma", slot)
        if n >= N_DMA_SEMS:
            w2 = []
            self._need(q, self.dma_tokens[n - N_DMA_SEMS], w2)
            waits = waits + w2
        tok = (key, 16 * gen, q)
        self.dma_tokens.append(tok)

        def fn(e, out=out, in_=in_, kw=kw):
            return e.dma_start(out=out, in_=in_, **kw)
        self.ops[q].append((waits, fn, (key, 16)))
        self._commit(tok, reads, writes)
        self.n_ops += 1
        return tok

    def cc(self, in_ap, out_ap, reads=(), writes=(), groups=None):
        q = "pool"
        waits = self._deps(q, reads, writes)
        n = self.dma_n
        self.dma_n += 1
        slot = n % N_DMA_SEMS
        gen = n // N_DMA_SEMS + 1
        key = ("dma", slot)
        if n >= N_DMA_SEMS:
            w2 = []
            self._need(q, self.dma_tokens[n - N_DMA_SEMS], w2)
            waits = waits + w2
        tok = (key, 16 * gen, q)
        self.dma_tokens.append(tok)

        def fn(e):
            return e.collective_compute("AllGather", ALU.bypass, replica_groups=groups or [list(range(8))],
                                        ins=[in_ap], outs=[out_ap])
        self.ops[q].append((waits, fn, (key, 16)))
        self._commit(tok, reads, writes)
        self.n_ops += 1
        return tok

    def barrier(self):
        toks = [(e, self.cnt[e], e) for e in COMPUTE if self.cnt[e] > 0]
        n = self.dma_n
        toks += [self.dma_tokens[i] for i in range(max(0, n - N_DMA_SEMS), n)]
        for e in ALLENG:
            for t in toks:
                if t[0] == e:
                    continue
                w = []
                self._need(e, t, w)
                self.pending[e] += w

    def finalize(self, nc):
        with contextlib.ExitStack() as st:
            sems = {}
            for e in COMPUTE:
                sems[e] = st.enter_context(nc.semaphore("s_" + e))
            for i in range(min(N_DMA_SEMS, max(self.dma_n, 1))):
                sems[("dma", i)] = st.enter_context(nc.semaphore("d%d" % i))
            block = st.enter_context(nc.Block())
            fin = []
            n = self.dma_n
            for i in range(max(0, n - N_DMA_SEMS), n):
                self._need("sp", self.dma_tokens[i], fin)
            d = {}
            for k, v in fin:
                d[k] = max(d.get(k, 0), v)
            fin = list(d.items())

            def emit(h, name):
                for waits, fn, inc in self.ops[name]:
                    for k, v in waits:
                        h.wait_ge(sems[k], v)
                    fn(h).then_inc(sems[inc[0]], inc[1])
                if name == "sp":
                    for k, v in fin:
                        h.wait_ge(sems[k], v)

            @block.sync
            def _(e):
                emit(e, "sp")

            @block.tensor
            def _(e):
                emit(e, "pe")

            @block.scalar
            def _(e):
                emit(e, "act")

            @block.vector
            def _(e):
                emit(e, "dve")

            @block.gpsimd
            def _(e):
                emit(e, "pool")


def _bl(ts):
    out = []
    for t in ts:
        if isinstance(t, TL):
            out += t.bufs
        elif isinstance(t, Buf):
            out.append(t)
        else:
            out += list(t)
    return out


class K:
    def __init__(self, P):
        self.P = P

    def mm(self, out, lhsT, rhs, start=True, stop=True, r=(), w=()):
        self.P.op("pe", lambda e: e.matmul(out, lhsT=lhsT, rhs=rhs, start=start, stop=stop), _bl(r), _bl(w))

    def tr(self, out, in_, ident, r=(), w=()):
        self.P.op("pe", lambda e: e.transpose(out, in_, ident), _bl(r), _bl(w))

    def act(self, out, in_, func, bias=None, scale=None, accum=None, r=(), w=(), eng="act"):
        kw = {}
        if bias is not None:
            kw["bias"] = bias
        if scale is not None:
            kw["scale"] = scale
        if accum is not None:
            kw["accum_out"] = accum
        self.P.op(eng, lambda e: e.activation(out=out, in_=in_, func=func, **kw), _bl(r), _bl(w))

    def tt(self, out, in0, in1, op, r=(), w=(), eng="dve"):
        self.P.op(eng, lambda e: e.tensor_tensor(out=out, in0=in0, in1=in1, op=op), _bl(r), _bl(w))

    def ts(self, out, in0, s1, op0, s2=None, op1=None, r=(), w=(), eng="dve"):
        if op1 is None:
            self.P.op(eng, lambda e: e.tensor_scalar(out=out, in0=in0, scalar1=s1, scalar2=None, op0=op0),
                      _bl(r), _bl(w))
        else:
            self.P.op(eng, lambda e: e.tensor_scalar(out=out, in0=in0, scalar1=s1, scalar2=s2, op0=op0, op1=op1),
                      _bl(r), _bl(w))

    def stt(self, out, in0, scalar, in1, op0, op1, r=(), w=()):
        self.P.op("dve", lambda e: e.scalar_tensor_tensor(out=out, in0=in0, scalar=scalar, in1=in1, op0=op0, op1=op1),
                  _bl(r), _bl(w))

    def cp(self, out, in_, r=(), w=(), eng="dve"):
        if eng == "act":
            self.P.op("act", lambda e: e.copy(out=out, in_=in_), _bl(r), _bl(w))
        else:
            self.P.op(eng, lambda e: e.tensor_copy(out=out, in_=in_), _bl(r), _bl(w))

    def red(self, out, in_, op, r=(), w=(), axis=None):
        ax = AX.X if axis is None else axis
        self.P.op("dve", lambda e: e.tensor_reduce(out=out, in_=in_, axis=ax, op=op), _bl(r), _bl(w))

    def recip(self, out, in_, r=(), w=()):
        self.P.op("dve", lambda e: e.reciprocal(out=out, in_=in_), _bl(r), _bl(w))

    def memset(self, out, val, w=(), eng="pool"):
        self.P.op(eng, lambda e: e.memset(out, val), [], _bl(w))

    def dma(self, out, in_, r=(), w=(), q="sp"):
        self.P.dma(out, in_, _bl(r), _bl(w), q=q)


class Arena:
    def __init__(self, ap_f32, nwords):
        self.ap = ap_f32
        self.n = nwords
        self.top = 0

    def alloc(self, shape_free, dtype=F32, nsub=1):
        ne = int(np.prod(shape_free))
        words = ne if dtype in (F32, I32) else (ne + 1) // 2
        words = (words + 1) // 2 * 2
        off = self.top
        self.top += words
        assert self.top <= self.n, "SBUF arena overflow %d > %d" % (self.top, self.n)
        ap = self.ap[:, off:off + words]
        if dtype != F32:
            ap = ap.bitcast(dtype)
        ap = ap[:, 0:ne]
        if len(shape_free) == 2:
            ap = ap.rearrange("p (a b) -> p a b", a=shape_free[0])
        elif len(shape_free) == 3:
            ap = ap.rearrange("p (a b c) -> p a b c", a=shape_free[0], b=shape_free[1])
        return TL(ap, nsub)

    def mark(self):
        return self.top

    def release(self, m):
        self.top = m


class Cfg:
    def __init__(self, ntok, layers=(0, 1, 2, 3), final_norm=True, skip_ffn=False, skip_mixer=False):
        self.ntok = ntok
        self.layers = layers
        self.final_norm = final_norm
        self.skip_ffn = skip_ffn
        self.skip_mixer = skip_mixer
        self.stop = 99
        self.astop = 99
        self.gstop = 99


def build(cfg):
    NTOK = cfg.ntok
    NB = NTOK // 128
    nc = bass.Bass("TRN2", target_bir_lowering=False)

    def din(name, shape, dt=F32):
        return nc.dram_tensor(name, list(shape), dt, kind="ExternalInput").ap()

    x = din("x", [NTOK, D])
    mem = din("mem", [MEM_LEN, D])
    pos = din("pos", [NTOK], I32)
    ln_mix = din("ln_mix", [4, D])
    ln_ffn = din("ln_ffn", [4, D])
    ln_mem = din("ln_mem", [D])
    w_mem_kv = din("w_mem_kv", [4, D, 512])
    w_out = din("w_out", [4, D, D])
    w_gu = din("w_gate_up", [4, D, 2 * DFF])
    w_dn = din("w_down", [4, DFF, D])
    g_win = din("gdn_w_in", [2, D, GDN_IN])
    g_conv = din("gdn_conv", [2, 4, 2304])
    g_alog = din("gdn_A_log", [2, 6])
    g_dtb = din("gdn_dt_bias", [2, 6])
    g_norm = din("gdn_norm", [2, 128])
    s_wq = din("swa_w_q", [2, D, D])
    s_sink = din("swa_sinks", [2, 12])
    ln_kv = din("ln_kv", [D])
    w_kv = din("w_kv", [D, 256])
    ln_final = din("ln_final", [D])
    out = nc.dram_tensor("out", [NTOK, D], F32, kind="ExternalOutput").ap()
    S_in = din("s_in", [2, 128, 6, 128])
    hist_in = din("hist_in", [2, 128, 18, 4])
    kt_in = din("kt_in", [128, 128], BF16)
    vt_in = din("vt_in", [128, 128], BF16)
    hv_in = din("hv", [1])
    S_out = nc.dram_tensor("s_out", [2, 128, 6, 128], F32, kind="ExternalOutput").ap()
    hist_out = nc.dram_tensor("hist_out", [2, 128, 18, 4], F32, kind="ExternalOutput").ap()
    kt_out = nc.dram_tensor("kt_out", [128, 128], BF16, kind="ExternalOutput").ap()
    vt_out = nc.dram_tensor("vt_out", [128, 128], BF16, kind="ExternalOutput").ap()
    carry_buf = Buf()
    Rd = nc.dram_tensor("r_scratch", [NTOK, D], F32, kind="Internal").ap()

    P = Prog()
    k = K(P)
    st = contextlib.ExitStack()
    ARW = 52600
    arena_t = st.enter_context(nc.sbuf_tensor("arena", [128, ARW], F32))
    psum_t = st.enter_context(nc.psum_tensor("psum", [128, 8, 512], F32))
    A = Arena(arena_t, ARW)
    PS = [TL(psum_t[:, i, :], 4, excl=True) for i in range(8)]

    def psbf(i):
        return psum_t[:, i, :].bitcast(BF16)

    Rbuf = [Buf() for _ in range(NB)]
    Obuf = [Buf() for _ in range(NB)]

    dI = A.alloc((128,), I32)
    k_ = k
    P.op("pool", lambda e: e.iota(dI.ap, pattern=[[1, 128]], base=0, channel_multiplier=-1), [], dI.b())
    dF = A.alloc((128,), F32)
    k.cp(dF.ap, dI.ap, r=[dI], w=[dF])
    ident_f = A.alloc((128,), F32)
    ident_b = A.alloc((128,), BF16)
    k.ts(ident_f.ap, dF.ap, 0.0, ALU.is_equal, r=[dF], w=[ident_f])
    k.cp(ident_b.ap, ident_f.ap, r=[ident_f], w=[ident_b])
    maskBigL = A.alloc((128,), F32)
    k.ts(maskBigL.ap, dF.ap, 0.0, ALU.is_ge, BIG, ALU.mult, r=[dF], w=[maskBigL])
    maskNegT = A.alloc((128,), F32)
    k.ts(maskNegT.ap, dF.ap, 0.0, ALU.is_lt, -BIG, ALU.mult, r=[dF], w=[maskNegT])
    triu_f = A.alloc((128,), F32)
    k.ts(triu_f.ap, dF.ap, 0.0, ALU.is_ge, r=[dF], w=[triu_f])
    ones_f = A.alloc((128,), F32)
    k.memset(ones_f.ap, 1.0, w=[ones_f])
    ones_b = A.alloc((128,), BF16)
    k.memset(ones_b.ap, 1.0, w=[ones_b])
    maskS = A.alloc((256,), F32)
    k.ts(maskS.ap[:, 0:128], dF.ap, 0.0, ALU.is_le, -BIG, ALU.mult, r=[dF], w=[maskS])
    k.ts(maskS.ap[:, 128:256], dF.ap, 0.0, ALU.is_gt, -BIG, ALU.mult, r=[dF], w=[maskS])
    maskS0 = A.alloc((256,), F32)
    hvt = A.alloc((2,), F32)
    k.dma(hvt.ap[:, 0:1], hv_in.partition_broadcast(128), w=[hvt])
    k.ts(hvt.ap[:, 1:2], hvt.ap[:, 0:1], -1.0, ALU.add, BIG, ALU.mult, r=[hvt], w=[hvt])
    k.ts(maskS0.ap[:, 0:128], maskS.ap[:, 0:128], hvt.ap[:, 1:2], ALU.add, r=[maskS, hvt], w=[maskS0])
    k.cp(maskS0.ap[:, 128:256], maskS.ap[:, 128:256], r=[maskS], w=[maskS0], eng="pool")
    mhalf = A.alloc((8,), F32)
    k.memset(mhalf.ap, -0.5, w=[mhalf])
    epsb = A.alloc((2,), F32)
    k.memset(epsb.ap, EPS, w=[epsb])

    def bcast_load(src_1d, n):
        t = A.alloc((n,), F32)
        k.dma(t.ap, src_1d.partition_broadcast(128), w=[t])
        return t


    def rms_rstd(xt_ap, xt_r, junk, ss, rstd, n=D, eps=EPS):
        k.act(junk.ap, xt_ap, AF.Square, accum=ss.ap[:, 0:1], r=xt_r, w=[junk, ss])
        k.act(rstd.ap[:, 1:2], ss.ap[:, 0:1], AF.Identity, bias=epsb.ap[:, 0:1], scale=1.0 / n, r=[ss, epsb], w=[rstd])
        k.tt(rstd.ap[:, 0:1], rstd.ap[:, 1:2], mhalf.ap[:, 0:1], ALU.pow, r=[rstd, mhalf], w=[rstd], eng="pool")

    def norm_to_T(xt_ap, xt_r, gain, hT_ap, hT_w, scr, psb):
        junk, ss, rstd, hb = scr
        rms_rstd(xt_ap, xt_r, junk, ss, rstd)
        k.stt(hb.ap, xt_ap, rstd.ap[:, 0:1], gain.ap, ALU.mult, ALU.mult, r=list(xt_r) + [rstd, gain], w=[hb])
        pb = psbf(psb)
        for c in range(8):
            k.tr(pb[:, c * 128:(c + 1) * 128], hb.ap[:, c * 128:(c + 1) * 128], ident_b.ap,
                 r=[hb, ident_b], w=[PS[psb]])
        k.cp(hT_ap, pb.rearrange("p (c t) -> p c t", c=8), r=[PS[psb]], w=hT_w, eng="act")

    def load_w_bf16(dst, src_ap):
        k.dma(dst.ap, src_ap, w=[dst], q="pool")

    def ffn_phase(l, last):
        m0 = A.mark()
        TT = 256
        NBT = TT // 128
        wgu = A.alloc((KC, 2 * DFF), BF16)
        wdn = A.alloc((FC, D), BF16)
        for c in range(KC):
            k.dma(wgu.ap[:, c, :], w_gu[l, c * 128:(c + 1) * 128, :], w=[wgu], q="pool")
        for c in range(FC):
            k.dma(wdn.ap[:, c, :], w_dn[l, c * 128:(c + 1) * 128, :], w=[wdn], q="pool")
        g_ffn = bcast_load(ln_ffn[l, :], D)
        g_lnfinal = bcast_load(ln_final, D) if last else None
        Rt = [A.alloc((NBT, D), F32, nsub=NBT) for _ in range(2)]
        hT = [A.alloc((KC, TT), BF16) for _ in range(2)]
        actT = A.alloc((FC, TT), BF16, nsub=FC)
        sg = [A.alloc((TT,), F32) for _ in range(2)]
        scr = (A.alloc((D,), BF16), A.alloc((2,), F32), A.alloc((2,), F32), A.alloc((D,), BF16))
        scr2 = (scr[0], A.alloc((2,), F32), A.alloc((2,), F32), None) if last else None
        ntile = NTOK // TT

        def load(i):
            for j in range(NBT):
                blk = i * NBT + j
                k.dma(Rt[i % 2].ap[:, j, :], Rd[blk * 128:(blk + 1) * 128, :], r=[Rbuf[blk]], w=Rt[i % 2].b(j))
        load(0)
        for i in range(ntile):
            if i + 1 < ntile:
                load(i + 1)
            rt = Rt[i % 2]
            ht = hT[i % 2]
            for j in range(NBT):
                norm_to_T(rt.ap[:, j, :], rt.b(j), g_ffn, ht.ap[:, :, j * 128:(j + 1) * 128], [ht], scr, 0)
            for c in range(FC):
                pg = PS[1 + (c % 2) * 2]
                pu = PS[2 + (c % 2) * 2]
                for kk in range(KC):
                    k.mm(pg.ap[:, 0:TT], wgu.ap[:, kk, c * 128:(c + 1) * 128], ht.ap[:, kk, :],
                         start=(kk == 0), stop=(kk == KC - 1), r=[wgu, ht], w=[pg])
                for kk in range(KC):
                    k.mm(pu.ap[:, 0:TT], wgu.ap[:, kk, DFF + c * 128:DFF + (c + 1) * 128], ht.ap[:, kk, :],
                         start=(kk == 0), stop=(kk == KC - 1), r=[wgu, ht], w=[pu])
                s = sg[c % 2]
                k.act(s.ap, pg.ap[:, 0:TT], AF.Silu, r=[pg], w=[s])
                k.tt(actT.ap[:, c, :], s.ap, pu.ap[:, 0:TT], ALU.mult, r=[s, pu], w=actT.b(c))
            for j in range(NBT):
                blk = i * NBT + j
                for n in range(2):
                    py = PS[5 + n]
                    for c in range(FC):
                        k.mm(py.ap, actT.ap[:, c, j * 128:(j + 1) * 128], wdn.ap[:, c, n * 512:(n + 1) * 512],
                             start=(c == 0), stop=(c == FC - 1), r=[wdn] + actT.b(c), w=[py])
                    k.tt(rt.ap[:, j, n * 512:(n + 1) * 512], rt.ap[:, j, n * 512:(n + 1) * 512], py.ap, ALU.add,
                         r=[py] + rt.b(j), w=rt.b(j))
                if last and cfg.final_norm:
                    junk, ss, rstd, ob = scr2
                    rms_rstd(rt.ap[:, j, :], rt.b(j), junk, ss, rstd)
                    k.stt(rt.ap[:, j, :], rt.ap[:, j, :], rstd.ap[:, 0:1], g_lnfinal.ap, ALU.mult, ALU.mult,
                          r=rt.b(j) + [rstd, g_lnfinal], w=rt.b(j))
                    k.dma(out[blk * 128:(blk + 1) * 128, :], rt.ap[:, j, :], r=rt.b(j), w=[Obuf[blk]])
                elif last:
                    k.dma(out[blk * 128:(blk + 1) * 128, :], rt.ap[:, j, :], r=rt.b(j), w=[Obuf[blk]])
                else:
                    k.dma(Rd[blk * 128:(blk + 1) * 128, :], rt.ap[:, j, :], r=rt.b(j), w=[Rbuf[blk]])
        P.barrier()
        A.release(m0)

    memKT = A.alloc((4, 2, MEM_LEN), BF16)
    memV = A.alloc((4, 2, 256), BF16)
    has_b = any(l >= 2 for l in cfg.layers)
    if has_b:
        ropeC = A.alloc((NB, 8), F32)
        ropeS = A.alloc((NB, 8), F32)
        KTd = nc.dram_tensor("kt_scratch", [128, 128 + NTOK], BF16, kind="Internal").ap()
        Vtd = nc.dram_tensor("vt_scratch", [128, NB + 1, 128], BF16, kind="Internal").ap()
        KVbuf = Buf()

    def setup_phase():
        m0 = A.mark()
        g_lnmem = bcast_load(ln_mem, D)
        mt = A.alloc((2, D), F32)
        memT = A.alloc((KC, MEM_LEN), BF16)
        scr = (A.alloc((D,), BF16), A.alloc((2,), F32), A.alloc((2,), F32), A.alloc((D,), BF16))
        for j in range(2):
            k.dma(mt.ap[:, j, :], mem[j * 128:(j + 1) * 128, :], w=[mt])
        for j in range(2):
            norm_to_T(mt.ap[:, j, :], [mt], g_lnmem, memT.ap[:, :, j * 128:(j + 1) * 128], [memT], scr, 0)
        wm = [A.alloc((KC, 512), BF16) for _ in range(2)]
        for l in range(4):
            w = wm[l % 2]
            k.dma(w.ap, w_mem_kv[l].rearrange("(c p) n -> p c n", p=128), w=[w], q="pool")
            for c in range(2):
                ps = PS[1 + c]
                for kk in range(KC):
                    k.mm(ps.ap[:, 0:MEM_LEN], w.ap[:, kk, c * 128:(c + 1) * 128], memT.ap[:, kk, :],
                         start=(kk == 0), stop=(kk == KC - 1), r=[w, memT], w=[ps])
                k.cp(memKT.ap[:, l, c, :], ps.ap[:, 0:MEM_LEN], r=[ps], w=[memKT])
            for mb in range(2):
                ps = PS[3 + mb]
                for kk in range(KC):
                    k.mm(ps.ap[:, 0:256], memT.ap[:, kk, mb * 128:(mb + 1) * 128], w.ap[:, kk, 256:512],
                         start=(kk == 0), stop=(kk == KC - 1), r=[w, memT], w=[ps])
                k.cp(memV.ap[:, l, mb, :], ps.ap[:, 0:256], r=[ps], w=[memV], eng="act")
        if has_b:
            pi_ = A.alloc((128,), I32)
            pf_ = A.alloc((128,), F32)
            if NB < 128:
                k.memset(pf_.ap, 0.0, w=[pf_])
            k.dma(pi_.ap[0:NB, :], pos.rearrange("(b p) -> b p", p=128), w=[pi_])
            k.cp(pf_.ap[0:NB, :], pi_.ap[0:NB, :], r=[pi_], w=[pf_])
            pT = PS[5]
            k.tr(pT.ap[:, 0:128], pf_.ap, ident_f.ap, r=[pf_, ident_f], w=[pT])
            posf = A.alloc((NB,), F32)
            k.cp(posf.ap, pT.ap[:, 0:NB], r=[pT], w=[posf])
            ang = A.alloc((NB, 8), F32)
            inv = (np.float32(500000.0) ** (-(np.arange(0, 16, 2).astype(np.float32)) / np.float32(16))).astype(np.float32)
            for j in range(8):
                k.ts(ang.ap[:, :, j], posf.ap, float(inv[j]), ALU.mult, r=[posf], w=[ang])
            tq = A.alloc((NB, 8), F32)
            ki = A.alloc((NB, 8), I32)
            kf = A.alloc((NB, 8), F32)
            TWO_PI = 2.0 * np.pi
            C1 = 6.28125
            C2 = TWO_PI - C1
            k.ts(tq.ap, ang.ap, 1.0 / TWO_PI, ALU.mult, r=[ang], w=[tq])
            k.cp(ki.ap, tq.ap, r=[tq], w=[ki])
            k.cp(kf.ap, ki.ap, r=[ki], w=[kf])
            rr = A.alloc((NB, 8), F32)
            k.stt(rr.ap, kf.ap, -C1, ang.ap, ALU.mult, ALU.add, r=[kf, ang], w=[rr])
            k.stt(rr.ap, kf.ap, -C2, rr.ap, ALU.mult, ALU.add, r=[kf, rr], w=[rr])
            k.ts(tq.ap, rr.ap, float(np.pi), ALU.is_gt, -TWO_PI, ALU.mult, r=[rr], w=[tq])
            k.tt(rr.ap, rr.ap, tq.ap, ALU.add, r=[rr, tq], w=[rr])
            k.ts(tq.ap, rr.ap, float(-np.pi), ALU.is_lt, TWO_PI, ALU.mult, r=[rr], w=[tq])
            k.tt(rr.ap, rr.ap, tq.ap, ALU.add, r=[rr, tq], w=[rr])
            k.act(ropeS.ap, rr.ap, AF.Sin, r=[rr], w=[ropeS])
            k.act(tq.ap, rr.ap, AF.Sin, scale=0.5, r=[rr], w=[tq])
            k.tt(tq.ap, tq.ap, tq.ap, ALU.mult, r=[tq], w=[tq])
            k.ts(ropeC.ap, tq.ap, -2.0, ALU.mult, 1.0, ALU.add, r=[tq], w=[ropeC])
        P.barrier()
        A.release(m0)

    def rope(dst4, src4, b, H1, H2, tmp, r, w):
        c = ropeC.ap[:, b, :].unsqueeze(1).unsqueeze(1).to_broadcast([128, H1, H2, 8])
        s_ = ropeS.ap[:, b, :].unsqueeze(1).unsqueeze(1).to_broadcast([128, H1, H2, 8])
        x1 = src4[:, :, :, 0:8]
        x2 = src4[:, :, :, 8:16]
        t1 = tmp.ap[:, 0:H1 * H2 * 8].rearrange("p (a b d) -> p a b d", a=H1, b=H2)
        t2 = tmp.ap[:, H1 * H2 * 8:2 * H1 * H2 * 8].rearrange("p (a b d) -> p a b d", a=H1, b=H2)
        rr_ = list(r) + [ropeC, ropeS]
        k.tt(t1, x1, c, ALU.mult, r=rr_, w=[tmp])
        k.tt(t2, x2, s_, ALU.mult, r=rr_, w=[tmp])
        k.tt(dst4[:, :, :, 0:8], t1, t2, ALU.subtract, r=[tmp], w=w)
        k.tt(t1, x2, c, ALU.mult, r=rr_, w=[tmp])
        k.tt(t2, x1, s_, ALU.mult, r=rr_, w=[tmp])
        k.tt(dst4[:, :, :, 8:16], t1, t2, ALU.add, r=[tmp], w=w)
        k.cp(dst4[:, :, :, 16:64], src4[:, :, :, 16:64], r=r, w=w, eng="pool" if False else "dve")

    def kv_phase():
        m0 = A.mark()
        g = bcast_load(ln_kv, D)
        wkv = A.alloc((KC, 256), BF16)
        k.dma(wkv.ap, w_kv.rearrange("(c p) n -> p c n", p=128), w=[wkv], q="pool")
        KTp = A.alloc((128 + NTOK,), BF16)
        Vtp = A.alloc((NB + 1, 128), BF16)
        k.dma(KTp.ap[:, 0:128], kt_in, w=[KTp])
        k.dma(Vtp.ap[:, 0, :], vt_in, w=[Vtp])
        Rt = [A.alloc((D,), F32) for _ in range(2)]
        hT = [A.alloc((KC, 128), BF16) for _ in range(2)]
        scr = (A.alloc((D,), BF16), A.alloc((2,), F32), A.alloc((2,), F32), A.alloc((D,), BF16))
        kvf = A.alloc((256,), F32)
        kb = A.alloc((128,), BF16)
        tmp = A.alloc((2 * 2 * 8,), F32)
        k.dma(Rt[0].ap, Rd[0:128, :], r=[Rbuf[0]], w=[Rt[0]])
        for b in range(NB):
            if b + 1 < NB:
                k.dma(Rt[(b + 1) % 2].ap, Rd[(b + 1) * 128:(b + 2) * 128, :], r=[Rbuf[b + 1]], w=[Rt[(b + 1) % 2]])
            rt, ht = Rt[b % 2], hT[b % 2]
            norm_to_T(rt.ap, [rt], g, ht.ap, [ht], scr, 0)
            ps = PS[1 + b % 2]
            for kk in range(KC):
                k.mm(ps.ap[:, 0:256], ht.ap[:, kk, :], wkv.ap[:, kk, :], start=(kk == 0), stop=(kk == KC - 1),
                     r=[ht, wkv], w=[ps])
            k.cp(kvf.ap, ps.ap[:, 0:256], r=[ps], w=[kvf], eng="act")
            k.cp(Vtp.ap[:, b + 1, :], kvf.ap[:, 128:256], r=[kvf], w=[Vtp], eng="act")
            src4 = kvf.ap[:, 0:128].rearrange("p (a b d) -> p a b d", a=1, b=2)
            dst4 = kb.ap.rearrange("p (a b d) -> p a b d", a=1, b=2)
            rope(dst4, src4, b, 1, 2, tmp, [kvf], [kb])
            pb = psbf(3 + b % 2)
            k.tr(pb[:, 0:128], kb.ap, ident_b.ap, r=[kb, ident_b], w=[PS[3 + b % 2]])
            k.cp(KTp.ap[:, 128 + b * 128:128 + (b + 1) * 128], pb[:, 0:128], r=[PS[3 + b % 2]], w=[KTp], eng="act")
        k.dma(KTd, KTp.ap, r=[KTp], w=[KVbuf])
        k.dma(Vtd, Vtp.ap, r=[Vtp], w=[KVbuf])
        k.dma(kt_out, KTp.ap[:, NTOK:NTOK + 128], r=[KTp], w=[carry_buf])
        k.dma(vt_out, Vtp.ap[:, NB, :], r=[Vtp], w=[carry_buf])
        P.barrier()
        A.release(m0)

    def attn_group(sc_ps, nh, nk, mask_ap, sink_ap, scale, lhs_list, rhs_list, v_list, o_ps_ap, o_dst4, o_src4,
                   S, extra_r, o_w):
        Sm, mx, nm, sums, es, PTt, Pb = S
        nkb = nk // 128
        per_bank = 512 // nk
        for h in range(nh):
            ps = sc_ps[h // per_bank]
            o = (h % per_bank) * nk
            k.mm(ps.ap[:, o:o + nk], lhs_list[h], rhs_list[h], r=extra_r, w=[ps])
        if cfg.astop <= 1:
            return
        for bi, ps in enumerate(sc_ps):
            h0 = bi * per_bank
            src = ps.ap.rearrange("p (h n) -> p h n", h=per_bank)
            dst = Sm.ap[:, h0:h0 + per_bank, :]
            if mask_ap is not None:
                k.tt(dst, src, mask_ap.unsqueeze(1).to_broadcast([128, per_bank, nk]), ALU.add, r=[ps] + list(extra_r), w=[Sm])
            else:
                k.cp(dst, src, r=[ps], w=[Sm])
        if cfg.astop <= 2:
            return
        k.red(mx.ap[:, 0:nh], Sm.ap[:, 0:nh, :], ALU.max, r=[Sm], w=[mx])
        if cfg.astop <= 3:
            return
        if sink_ap is not None:
            k.stt(mx.ap[:, 0:nh], mx.ap[:, 0:nh], scale, sink_ap, ALU.mult, ALU.max, r=[mx] + list(extra_r), w=[mx])
            k.ts(nm.ap[:, 0:nh], mx.ap[:, 0:nh], -1.0, ALU.mult, r=[mx], w=[nm])
            k.tt(es.ap[:, 0:nh], sink_ap, mx.ap[:, 0:nh], ALU.subtract, r=[mx] + list(extra_r), w=[es])
            k.act(es.ap[:, 0:nh], es.ap[:, 0:nh], AF.Exp, r=[es], w=[es])
        else:
            k.ts(nm.ap[:, 0:nh], mx.ap[:, 0:nh], -scale, ALU.mult, r=[mx], w=[nm])
        if cfg.astop <= 4:
            return
        for h in range(nh):
            k.act(Pb.ap[:, h, :], Sm.ap[:, h, :], AF.Exp, bias=nm.ap[:, h:h + 1], scale=scale,
                  accum=sums.ap[:, h:h + 1], r=[Sm, nm], w=[Pb, sums])
        k.act(mx.ap[:, 0:nh], sums.ap[:, 0:nh], AF.Copy, r=[sums], w=[mx])
        if sink_ap is not None:
            k.tt(mx.ap[:, 0:nh], mx.ap[:, 0:nh], es.ap[:, 0:nh], ALU.add, r=[mx, es], w=[mx])
        k.recip(sums.ap[:, 0:nh], mx.ap[:, 0:nh], r=[mx], w=[sums])
        if cfg.astop <= 5:
            return
        pb = psbf(0)
        for h in range(nh):
            for kb_ in range(nkb):
                i = h * nkb + kb_
                k.tr(pb[:, i * 128:(i + 1) * 128], Pb.ap[:, h, kb_ * 128:(kb_ + 1) * 128], ident_b.ap,
                     r=[Pb, ident_b], w=[PS[0]])
        n_t = nh * nkb
        k.cp(PTt.ap[:, 0:n_t, :], pb[:, 0:n_t * 128].rearrange("p (c t) -> p c t", c=n_t), r=[PS[0]], w=[PTt], eng="act")
        if cfg.astop <= 6:
            return
        for h in range(nh):
            for kb_ in range(nkb):
                k.mm(o_ps_ap[:, h * 64:(h + 1) * 64], PTt.ap[:, h * nkb + kb_, :], v_list[h][kb_],
                     start=(kb_ == 0), stop=(kb_ == nkb - 1), r=[PTt] + list(extra_r), w=[PS[7]])
        if cfg.astop <= 7:
            return
        k.tt(o_dst4, o_src4, sums.ap[:, 0:nh].rearrange("p (a b) -> p a b", a=o_dst4.shape[1]).unsqueeze(3)
             .to_broadcast(list(o_dst4.shape)), ALU.mult, r=[PS[7], sums], w=o_w)

    def alloc_attn_scratch():
        return (A.alloc((4, 256), F32), A.alloc((4,), F32), A.alloc((4,), F32), A.alloc((4,), F32),
                A.alloc((4,), F32), A.alloc((8, 128), BF16), A.alloc((4, 256), BF16))

    def mem_attn(l, b_unused, mqT, mix_tok, S, o_half, scb=None):
        lhs = [mqT.ap[:, m % 2, m // 2, :] for m in range(4)]
        rhs = [memKT.ap[:, l, m // 2, :] for m in range(4)]
        vl = [[memV.ap[:, l, mb, m * 64:(m + 1) * 64] for mb in range(2)] for m in range(4)]
        o_ps = PS[7].ap[:, o_half * 256:(o_half + 1) * 256]
        o_src4 = o_ps.rearrange("p (a b d) -> p a b d", a=4, b=1)
        o_dst4 = mix_tok.ap[:, 768:1024].rearrange("p (a b d) -> p a b d", a=4, b=1)
        attn_group(scb or [PS[5], PS[6]], 4, 256, None, None, 0.125, lhs, rhs, vl, o_ps, o_dst4, o_src4, S,
                   [mqT, memKT, memV], [mix_tok])

    def out_proj_block(mix_tok, mixT, wo, rt, rt_b, blk, store=True):
        pb = psbf(0)
        for c in range(8):
            k.tr(pb[:, c * 128:(c + 1) * 128], mix_tok.ap[:, c * 128:(c + 1) * 128], ident_b.ap,
                 r=[mix_tok, ident_b], w=[PS[0]])
        k.cp(mixT.ap, pb.rearrange("p (c t) -> p c t", c=8), r=[PS[0]], w=[mixT], eng="act")
        for n in range(2):
            py = PS[1 + n]
            for c in range(KC):
                k.mm(py.ap, mixT.ap[:, c, :], wo.ap[:, c, n * 512:(n + 1) * 512], start=(c == 0), stop=(c == KC - 1),
                     r=[mixT, wo], w=[py])
            k.tt(rt[:, n * 512:(n + 1) * 512], rt[:, n * 512:(n + 1) * 512], py.ap, ALU.add, r=[py] + rt_b, w=rt_b)
        if store:
            k.dma(Rd[blk * 128:(blk + 1) * 128, :], rt, r=rt_b, w=[Rbuf[blk]])

    def swa_phase(l):
        bl = l - 2
        m0 = A.mark()
        g = bcast_load(ln_mix[l, :], D)
        wq = A.alloc((KC, D), BF16)
        wo = A.alloc((KC, D), BF16)
        k.dma(wq.ap, s_wq[bl].rearrange("(c p) n -> p c n", p=128), w=[wq], q="pool")
        k.dma(wo.ap, w_out[l].rearrange("(c p) n -> p c n", p=128), w=[wo], q="pool")
        KTp = A.alloc((128 + NTOK,), BF16)
        Vtp = A.alloc((NB + 1, 128), BF16)
        k.dma(KTp.ap, KTd, r=[KVbuf], w=[KTp])
        k.dma(Vtp.ap, Vtd, r=[KVbuf], w=[Vtp])
        skp = A.alloc((12,), F32)
        for g_ in range(2):
            P.dma(skp.ap.rearrange("p (j g) -> p j g", g=2)[:, :, g_], s_sink[bl, g_ * 6:(g_ + 1) * 6].partition_broadcast(128),
                  [], skp.b(), allow_slow_non_contiguous=True)
        Rt = [A.alloc((D,), F32) for _ in range(2)]
        hT = A.alloc((KC, 128), BF16)
        scr = (A.alloc((D,), BF16), A.alloc((2,), F32), A.alloc((2,), F32), A.alloc((D,), BF16))
        qf = A.alloc((D,), F32)
        qb = A.alloc((D,), BF16)
        qT = A.alloc((2, KC, 128), BF16)
        k.memset(qT.ap, 0.0, w=[qT])
        tmp = A.alloc((2 * 12 * 8,), F32)
        mix_tok = A.alloc((D,), BF16)
        mixT = A.alloc((KC, 128), BF16)
        S = alloc_attn_scratch()
        k.dma(Rt[0].ap, Rd[0:128, :], r=[Rbuf[0]], w=[Rt[0]])
        for b in range(NB):
            if b + 1 < NB:
                k.dma(Rt[(b + 1) % 2].ap, Rd[(b + 1) * 128:(b + 2) * 128, :], r=[Rbuf[b + 1]], w=[Rt[(b + 1) % 2]])
            rt = Rt[b % 2]
            norm_to_T(rt.ap, [rt], g, hT.ap, [hT], scr, 0)
            for n in range(2):
                ps = PS[1 + n]
                for kk in range(KC):
                    k.mm(ps.ap, hT.ap[:, kk, :], wq.ap[:, kk, n * 512:(n + 1) * 512], start=(kk == 0), stop=(kk == KC - 1),
                         r=[hT, wq], w=[ps])
                k.cp(qf.ap[:, n * 512:(n + 1) * 512], ps.ap, r=[ps], w=[qf], eng="act")
            if cfg.stop <= 1:
                continue
            src4 = qf.ap[:, 0:768].rearrange("p (g j d) -> p g j d", g=2, j=6)
            dst4 = qb.ap[:, 0:768].rearrange("p (j g d) -> p g j d", g=2, j=6)
            rope(dst4, src4, b, 2, 6, tmp, [qf], [qb])
            k.cp(qb.ap[:, 768:1024], qf.ap[:, 768:1024], r=[qf], w=[qb], eng="act")
            if cfg.stop <= 2:
                continue
            pb = psbf(0)
            for c in range(8):
                k.tr(pb[:, c * 128:(c + 1) * 128], qb.ap[:, c * 128:(c + 1) * 128], ident_b.ap, r=[qb, ident_b], w=[PS[0]])
            pbv = pb.rearrange("p (c t) -> p c t", c=8)
            k.cp(qT.ap[0:64, 0, :, :], pbv[0:64], r=[PS[0]], w=[qT], eng="act")
            k.cp(qT.ap[64:128, 1, :, :], pbv[64:128], r=[PS[0]], w=[qT], eng="dve")
            mask = maskS0 if b == 0 else maskS
            if cfg.stop <= 3:
                continue
            for gi in range(3):
                heads = [(2 * gi + jj, g_) for jj in range(2) for g_ in range(2)]
                lhs = [qT.ap[:, g_, j, :] for (j, g_) in heads]
                rhs = [KTp.ap[:, b * 128:(b + 2) * 128] for (j, g_) in heads]
                vl = [[Vtp.ap[:, b + kb_, g_ * 64:(g_ + 1) * 64] for kb_ in range(2)] for (j, g_) in heads]
                half = gi % 2
                o_ps = PS[7].ap[:, half * 256:(half + 1) * 256]
                o_src4 = o_ps.rearrange("p (j g d) -> p j g d", j=2, g=2)
                o_dst4 = mix_tok.ap[:, 0:768].rearrange("p (g j d) -> p j g d", g=2, j=6)[:, 2 * gi:2 * gi + 2, :, :]
                scb = [PS[3], PS[4]] if gi % 2 == 0 else [PS[5], PS[6]]
                attn_group(scb, 4, 256, mask.ap, skp.ap[:, 4 * gi:4 * gi + 4], 0.125, lhs, rhs, vl, o_ps, o_dst4, o_src4,
                           S, [qT, KTp, Vtp, mask, skp], [mix_tok])
            if cfg.stop <= 4:
                continue
            mq = TL(qT.ap[:, :, 6:8, :])
            mq.bufs = qT.bufs
            mem_attn(l, b, mq, mix_tok, S, 1)
            if cfg.stop <= 5:
                continue
            out_proj_block(mix_tok, mixT, wo, rt.ap, rt.b(), b)
        P.barrier()
        A.release(m0)

    def fin_phase():
        m0 = A.mark()
        g_lnfinal = bcast_load(ln_final, D)
        t = [A.alloc((D,), F32) for _ in range(2)]
        scr2 = (A.alloc((D,), BF16), A.alloc((2,), F32), A.alloc((2,), F32), A.alloc((D,), F32))
        for b in range(NB):
            tt_ = t[b % 2]
            k.dma(tt_.ap, Rd[b * 128:(b + 1) * 128, :], r=[Rbuf[b]], w=[tt_])
            if cfg.final_norm:
                junk, ss, rstd, ob = scr2
                rms_rstd(tt_.ap, [tt_], junk, ss, rstd)
                k.stt(ob.ap, tt_.ap, rstd.ap[:, 0:1], g_lnfinal.ap, ALU.mult, ALU.mult, r=[tt_, rstd, g_lnfinal], w=[ob])
                k.dma(out[b * 128:(b + 1) * 128, :], ob.ap, r=[ob], w=[Obuf[b]])
            else:
                k.dma(out[b * 128:(b + 1) * 128, :], tt_.ap, r=[tt_], w=[Obuf[b]])
        P.barrier()
        A.release(m0)

    def gdn_phase(l):
        a = l
        m0 = A.mark()
        TM = min(256, NTOK)
        NBM = TM // 128
        g = bcast_load(ln_mix[l, :], D)
        win = A.alloc((KC, GDN_IN), BF16)
        wo = A.alloc((KC, D), BF16)
        for c in range(KC):
            k.dma(win.ap[:, c, :], g_win[a, c * 128:(c + 1) * 128, :], w=[win], q="pool")
        k.dma(wo.ap, w_out[l].rearrange("(c p) n -> p c n", p=128), w=[wo], q="pool")
        cwr = A.alloc((4, 128), F32)
        k.dma(cwr.ap[0:18], g_conv[a].rearrange("j (c p) -> c j p", p=128), w=[cwr])
        cw = A.alloc((4, 18), F32)
        for j in range(4):
            k.tr(PS[3].ap[:, j * 32:j * 32 + 18], cwr.ap[0:18, j, :], ident_f.ap[0:18, 0:18], r=[cwr, ident_f], w=[PS[3]])
        k.cp(cw.ap, PS[3].ap[:, 0:128].rearrange("p (j c) -> p j c", j=4)[:, :, 0:18], r=[PS[3]], w=[cw])
        alog = bcast_load(g_alog[a, :], 6)
        dtb = bcast_load(g_dtb[a, :], 6)
        negA = A.alloc((6,), F32)
        k.act(negA.ap, alog.ap, AF.Exp, r=[alog], w=[negA])
        k.ts(negA.ap, negA.ap, -1.0, ALU.mult, r=[negA], w=[negA])
        ngb = bcast_load(g_norm[a, :], 128)
        k.ts(ngb.ap, ngb.ap, 0.5, ALU.mult, r=[ngb], w=[ngb])
        seli = A.alloc((6, 128), I32)
        sel = A.alloc((6, 128), F32)
        P.op("pool", lambda e: e.iota(seli.ap[0:6], pattern=[[1, 6], [0, 128]], base=0, channel_multiplier=-1), [], seli.b())
        k.ts(sel.ap[0:6], seli.ap[0:6], 0.0, ALU.is_equal, r=[seli], w=[sel])
        if cfg.gstop <= 1:
            P.barrier(); A.release(m0); return
        Rt = [A.alloc((D,), F32) for _ in range(2)]
        Ro = [A.alloc((D,), F32) for _ in range(2)]
        hT = A.alloc((KC, TM), BF16)
        scr = (A.alloc((D,), BF16), A.alloc((2,), F32), A.alloc((2,), F32), A.alloc((D,), BF16))
        cbuf = [A.alloc((TM + 4,), F32) for _ in range(2)]
        hist = A.alloc((18, 4), F32)
        k.dma(hist.ap, hist_in[a], w=[hist])
        acc = [A.alloc((TM,), F32) for _ in range(2)]
        tcv = [A.alloc((TM,), F32) for _ in range(2)]
        th = [A.alloc((TM,), F32) for _ in range(2)]
        sqb = [A.alloc((TM,), BF16) for _ in range(2)]
        rr = [A.alloc((TM,), F32) for _ in range(2)]
        qkvT = A.alloc((18, TM), BF16, nsub=18)
        mqT = A.alloc((2, 2, TM), BF16)
        k.memset(mqT.ap, 0.0, w=[mqT])
        ztok = A.alloc((NBM, 780), F32, nsub=NBM)
        gsc = A.alloc((10, NBM * 6), F32)
        gt = A.alloc((NBM, 6), F32)
        bt = A.alloc((NBM, 6), F32)
        Sf = A.alloc((6, 128), F32, nsub=6)
        Sb = A.alloc((6, 128), BF16, nsub=6)
        k.dma(Sf.ap, S_in[a], w=[Sf])
        k.cp(Sb.ap, Sf.ap, r=[Sf], w=[Sb])
        gcs = A.alloc((6,), F32)
        gcT = A.alloc((128,), F32)
        eg = A.alloc((6,), F32)
        ekl = A.alloc((6,), F32)
        gl = A.alloc((6,), F32)
        beg = A.alloc((6,), F32)
        bh = A.alloc((6,), F32)
        gtmp = A.alloc((6,), F32)
        NPAR = 3
        HBANKS = [(1, 2), (3, 4), (5, 6)]
        HS = []
        for _ in range(NPAR):
            d = {}
            for nm_ in ("Dm0", "DmL", "DmT", "decL", "decT", "ebc", "L", "LT", "M0", "M1", "MT0", "MT1", "TT"):
                d[nm_] = A.alloc((128,), F32)
            for nm_ in ("TTb", "AT", "QgT", "Kbg", "Kg", "Vb", "nWT", "Vn"):
                d[nm_] = A.alloc((128,), BF16)
            HS.append(d)
        Ot = A.alloc((6, 128), F32, nsub=6)
        ssO = A.alloc((6,), F32)
        rsO = A.alloc((6,), F32)
        ojunk = [A.alloc((128,), BF16) for _ in range(NPAR)]
        tz = A.alloc((768,), F32)
        gz = A.alloc((768,), F32)
        mix_tok = A.alloc((D,), BF16)
        mixT = A.alloc((KC, 128), BF16)
        S = alloc_attn_scratch()
        nmac = NTOK // TM
        Q = lambda i, q_: PS[i].b(q_)

        def psq(i, q_):
            return PS[i].ap[:, q_ * 128:(q_ + 1) * 128]

        def psqb(i, q_):
            return PS[i].ap[:, q_ * 128:q_ * 128 + 64].bitcast(BF16)

        k.dma(Rt[0].ap, Rd[0:128, :], r=[Rbuf[0]], w=[Rt[0]])
        for mi in range(nmac):
            for j in range(NBM):
                blk = mi * NBM + j
                if blk + 1 < NB:
                    k.dma(Rt[(blk + 1) % 2].ap, Rd[(blk + 1) * 128:(blk + 2) * 128, :], r=[Rbuf[blk + 1]], w=[Rt[(blk + 1) % 2]])
                rt = Rt[blk % 2]
                norm_to_T(rt.ap, [rt], g, hT.ap[:, :, j * 128:(j + 1) * 128], [hT], scr, 0)
            if cfg.gstop <= 2:
                continue
            for c in range(20):
                ps = PS[1 + c % 2]
                col = c * 128 if c < 18 else 3084 + (c - 18) * 128
                for kk in range(KC):
                    k.mm(ps.ap[:, 0:TM], win.ap[:, kk, col:col + 128], hT.ap[:, kk, :], start=(kk == 0), stop=(kk == KC - 1),
                         r=[win, hT], w=[ps])
                if c >= 18:
                    cc = c - 18
                    k.cp(mqT.ap[0:64, 0, cc, :], ps.ap[0:64, 0:TM], r=[ps], w=[mqT], eng="act")
                    k.cp(mqT.ap[64:128, 1, cc, :], ps.ap[64:128, 0:TM], r=[ps], w=[mqT], eng="dve")
                    continue
                cb = cbuf[c % 2]
                k.cp(cb.ap[:, 0:3], hist.ap[:, c, 0:3], r=[hist], w=[cb], eng="dve")
                k.cp(cb.ap[:, 3:3 + TM], ps.ap[:, 0:TM], r=[ps], w=[cb], eng="act")
                k.cp(hist.ap[:, c, 0:3], cb.ap[:, TM:TM + 3], r=[cb], w=[hist], eng="dve")
                ac, tc = acc[c % 2], tcv[c % 2]
                k.ts(ac.ap, cb.ap[:, 0:TM], cw.ap[:, 0, c:c + 1], ALU.mult, r=[cb, cw], w=[ac])
                for tp in range(1, 4):
                    k.stt(ac.ap, cb.ap[:, tp:tp + TM], cw.ap[:, tp, c:c + 1], ac.ap, ALU.mult, ALU.add, r=[cb, cw, ac], w=[ac])
                t_ = th[c % 2]
                k.act(t_.ap, ac.ap, AF.Tanh, scale=0.5, r=[ac], w=[t_])
                if c >= 12:
                    k.stt(qkvT.ap[:, c, :], t_.ap, 1.0, ac.ap, ALU.add, ALU.mult, r=[t_, ac], w=qkvT.b(c))
                    continue
                k.stt(t_.ap, t_.ap, 1.0, ac.ap, ALU.add, ALU.mult, r=[t_, ac], w=[t_])
                sq = sqb[c % 2]
                k.tt(sq.ap, t_.ap, t_.ap, ALU.mult, r=[t_], w=[sq])
                pss = PS[3]
                k.mm(pss.ap[:, 0:TM], ones_b.ap, sq.ap, r=[ones_b, sq], w=[pss])
                r_ = rr[c % 2]
                k.act(r_.ap, pss.ap[:, 0:TM], AF.Sqrt, bias=epsb.ap[:, 0:1], scale=0.25, r=[pss, epsb], w=[r_])
                k.recip(r_.ap, r_.ap, r=[r_], w=[r_])
                kap = 0.5 * (128.0 ** -0.5) if c < 6 else 0.5
                k.stt(qkvT.ap[:, c, :], t_.ap, kap, r_.ap, ALU.mult, ALU.mult, r=[t_, r_], w=qkvT.b(c))
            if cfg.gstop <= 3:
                continue
            for j in range(NBM):
                for n in range(2):
                    ps = PS[1 + n]
                    wdt = 512 if n == 0 else 268
                    for kk in range(KC):
                        k.mm(ps.ap[:, 0:wdt], hT.ap[:, kk, j * 128:(j + 1) * 128], win.ap[:, kk, 2304 + n * 512:2304 + n * 512 + wdt],
                             start=(kk == 0), stop=(kk == KC - 1), r=[win, hT], w=[ps])
                    k.cp(ztok.ap[:, j, n * 512:n * 512 + wdt], ps.ap[:, 0:wdt], r=[ps], w=ztok.b(j), eng="act")
            if cfg.gstop <= 4:
                continue
            def G_(i):
                return gsc.ap[:, i, :].rearrange("p (j h) -> p j h", j=NBM)
            zl = ztok.ap[:, :, 768:774]
            al = ztok.ap[:, :, 774:780]
            dtb3 = dtb.ap.unsqueeze(1).to_broadcast([128, NBM, 6])
            negA3 = negA.ap.unsqueeze(1).to_broadcast([128, NBM, 6])
            k.tt(G_(0), al, dtb3, ALU.add, r=[ztok, dtb], w=[gsc])
            k.ts(G_(1), G_(0), -1.0, ALU.mult, r=[gsc], w=[gsc])
            k.tt(G_(1), G_(1), G_(0), ALU.max, r=[gsc], w=[gsc])
            k.act(G_(2), G_(1), AF.Exp, scale=-1.0, r=[gsc], w=[gsc])
            k.ts(G_(2), G_(2), 1.0, ALU.add, r=[gsc], w=[gsc])
            k.act(G_(3), G_(2), AF.Ln, r=[gsc], w=[gsc])
            k.stt(G_(4), G_(0), 0.0, G_(3), ALU.max, ALU.add, r=[gsc], w=[gsc])
            k.tt(gt.ap, G_(4), negA3, ALU.mult, r=[gsc, negA], w=[gt])
            k.act(G_(5), zl, AF.Exp, scale=-1.0, r=[ztok], w=[gsc])
            k.ts(G_(5), G_(5), 1.0, ALU.add, r=[gsc], w=[gsc])
            k.recip(bt.ap, G_(5), r=[gsc], w=[bt])
            if cfg.gstop <= 5:
                continue
            for j in range(NBM):
                blk = mi * NBM + j
                cs_ = slice(j * 128, (j + 1) * 128)
                k.dma(Ro[blk % 2].ap, Rd[blk * 128:(blk + 1) * 128, :], r=[Rbuf[blk]], w=[Ro[blk % 2]])
                pg = PS[7]
                k.mm(pg.ap[:, 0:6], triu_f.ap, gt.ap[:, j, :], r=[triu_f, gt], w=Q(7, 0))
                k.mm(pg.ap[:, 8:14], ones_f.ap, gt.ap[:, j, :], r=[ones_f, gt], w=Q(7, 0))
                if cfg.gstop <= 5.1:
                    continue
                k.cp(gcs.ap, pg.ap[:, 0:6], r=Q(7, 0), w=[gcs])
                if cfg.gstop <= 5.2:
                    continue
                k.act(eg.ap, pg.ap[:, 0:6], AF.Exp, r=Q(7, 0), w=[eg])
                k.act(gl.ap, pg.ap[:, 8:14], AF.Exp, r=Q(7, 0), w=[gl])
                if cfg.gstop <= 5.3:
                    continue
                k.tt(gtmp.ap, pg.ap[:, 8:14], gcs.ap, ALU.subtract, r=Q(7, 0) + [gcs], w=[gtmp])
                k.act(ekl.ap, gtmp.ap, AF.Exp, r=[gtmp], w=[ekl])
                k.tt(beg.ap, bt.ap[:, j, :], eg.ap, ALU.mult, r=[bt, eg], w=[beg])
                k.ts(bh.ap, bt.ap[:, j, :], 0.5, ALU.mult, r=[bt], w=[bh])
                if cfg.gstop <= 5.4:
                    continue
                k.tr(PS[7].ap[0:6, 128:256], gcs.ap, ident_f.ap, r=[gcs, ident_f], w=Q(7, 1))
                k.cp(gcT.ap[0:6, :], PS[7].ap[0:6, 128:256], r=Q(7, 1), w=[gcT])
                if cfg.gstop <= 6:
                    continue
                def head_ops(h, H, Pb, Qb):
                    kT = qkvT.ap[:, 6 + h, cs_]
                    qT_ = qkvT.ap[:, h, cs_]
                    vT_ = qkvT.ap[:, 12 + h, cs_]
                    k.mm(psq(Pb, 0), sel.ap[0:6, h, :], gcT.ap[0:6, :], r=[sel, gcT], w=Q(Pb, 0))
                    yield
                    k.stt(H["DmL"].ap, psq(Pb, 0), gcs.ap[:, h:h + 1], maskBigL.ap, ALU.subtract, ALU.add,
                          r=Q(Pb, 0) + [gcs, maskBigL], w=[H["DmL"]])
                    k.stt(H["DmT"].ap, psq(Pb, 0), gcs.ap[:, h:h + 1], maskNegT.ap, ALU.subtract, ALU.add,
                          r=Q(Pb, 0) + [gcs, maskNegT], w=[H["DmT"]])
                    k.act(H["ebc"].ap, psq(Pb, 0), AF.Exp, r=Q(Pb, 0), w=[H["ebc"]])
                    yield
                    k.act(H["decL"].ap, H["DmL"].ap, AF.Exp, scale=-1.0, r=[H["DmL"]], w=[H["decL"]])
                    k.act(H["decT"].ap, H["DmT"].ap, AF.Exp, r=[H["DmT"]], w=[H["decT"]])
                    yield
                    k.mm(psq(Pb, 1), kT, kT, r=qkvT.b(6 + h), w=Q(Pb, 1))
                    k.mm(psq(Pb, 2), kT, qT_, r=qkvT.b(6 + h) + qkvT.b(h), w=Q(Pb, 2))
                    yield
                    k.stt(H["L"].ap, psq(Pb, 1), bt.ap[:, j, h:h + 1], H["decL"].ap, ALU.mult, ALU.mult,
                          r=Q(Pb, 1) + [bt, H["decL"]], w=[H["L"]])
                    k.tt(H["AT"].ap, psq(Pb, 2), H["decT"].ap, ALU.mult, r=Q(Pb, 2) + [H["decT"]], w=[H["AT"]])
                    k.tt(H["QgT"].ap, qT_, H["ebc"].ap, ALU.mult, r=qkvT.b(h) + [H["ebc"]], w=[H["QgT"]])
                    yield
                    k.tr(psq(Pb, 3), H["L"].ap, ident_f.ap, r=[H["L"], ident_f], w=Q(Pb, 3))
                    yield
                    k.cp(H["LT"].ap, psq(Pb, 3), r=Q(Pb, 3), w=[H["LT"]], eng="act")
                    k.stt(H["TT"].ap, psq(Pb, 3), -1.0, ident_f.ap, ALU.mult, ALU.add, r=Q(Pb, 3) + [ident_f], w=[H["TT"]])
                    yield
                    M, MT = H["L"], H["LT"]
                    for lv in range(6):
                        Mn = H["M%d" % (lv % 2)]
                        MTn = H["MT%d" % (lv % 2)]
                        k.mm(psq(Pb, 0), MT.ap, M.ap, r=[MT, M], w=Q(Pb, 0))
                        if lv < 5:
                            k.mm(psq(Pb, 1), M.ap, MT.ap, r=[MT, M], w=Q(Pb, 1))
                        yield
                        k.cp(Mn.ap, psq(Pb, 0), r=Q(Pb, 0), w=[Mn], eng="act")
                        if lv < 5:
                            k.cp(MTn.ap, psq(Pb, 1), r=Q(Pb, 1), w=[MTn], eng="dve")
                        yield
                        k.mm(psq(Pb, 2), Mn.ap, H["TT"].ap, r=[Mn, H["TT"]], w=Q(Pb, 2))
                        yield
                        k.tt(H["TT"].ap, H["TT"].ap, psq(Pb, 2), ALU.add, r=Q(Pb, 2) + [H["TT"]], w=[H["TT"]])
                        yield
                        M, MT = Mn, MTn
                    k.cp(H["TTb"].ap, H["TT"].ap, r=[H["TT"]], w=[H["TTb"]], eng="act")
                    k.tr(psqb(Qb, 0), kT, ident_b.ap, r=qkvT.b(6 + h) + [ident_b], w=Q(Qb, 0))
                    k.tr(psqb(Qb, 1), vT_, ident_b.ap, r=qkvT.b(12 + h) + [ident_b], w=Q(Qb, 1))
                    yield
                    k.ts(H["Kbg"].ap, psqb(Qb, 0), beg.ap[:, h:h + 1], ALU.mult, r=Q(Qb, 0) + [beg], w=[H["Kbg"]])
                    k.act(H["Kg"].ap, psqb(Qb, 0), AF.Copy, scale=ekl.ap[:, h:h + 1], r=Q(Qb, 0) + [ekl], w=[H["Kg"]])
                    k.ts(H["Vb"].ap, psqb(Qb, 1), bh.ap[:, h:h + 1], ALU.mult, r=Q(Qb, 1) + [bh], w=[H["Vb"]])
                    yield
                    k.mm(psq(Qb, 2), H["Kbg"].ap, H["TTb"].ap, r=[H["Kbg"], H["TTb"]], w=Q(Qb, 2))
                    yield
                    k.act(H["nWT"].ap, psq(Qb, 2), AF.Copy, scale=-1.0, r=Q(Qb, 2), w=[H["nWT"]])
                    yield
                    k.mm(psq(Qb, 3), H["TTb"].ap, H["Vb"].ap, start=True, stop=False, r=[H["TTb"], H["Vb"]], w=Q(Qb, 3))
                    k.mm(psq(Qb, 3), H["nWT"].ap, Sb.ap[:, h, :], start=False, stop=True, r=[H["nWT"]] + Sb.b(h), w=Q(Qb, 3))
                    yield
                    k.cp(H["Vn"].ap, psq(Qb, 3), r=Q(Qb, 3), w=[H["Vn"]], eng="act")
                    yield
                    k.mm(psq(Pb, 3), H["QgT"].ap, Sb.ap[:, h, :], start=True, stop=False, r=[H["QgT"]] + Sb.b(h), w=Q(Pb, 3))
                    k.mm(psq(Pb, 3), H["AT"].ap, H["Vn"].ap, start=False, stop=True, r=[H["AT"], H["Vn"]], w=Q(Pb, 3))
                    k.mm(psq(Pb, 0), H["Kg"].ap, H["Vn"].ap, r=[H["Kg"], H["Vn"]], w=Q(Pb, 0))
                    yield
                    k.act(ojunk[h % NPAR].ap, psq(Pb, 3), AF.Square, accum=ssO.ap[:, h:h + 1], r=Q(Pb, 3), w=[ojunk[h % NPAR], ssO])
                    k.cp(Ot.ap[:, h, :], psq(Pb, 3), r=Q(Pb, 3), w=Ot.b(h))
                    k.stt(Sf.ap[:, h, :], Sf.ap[:, h, :], gl.ap[:, h:h + 1], psq(Pb, 0), ALU.mult, ALU.add,
                          r=Q(Pb, 0) + [gl] + Sf.b(h), w=Sf.b(h))
                    yield
                    k.cp(Sb.ap[:, h, :], Sf.ap[:, h, :], r=Sf.b(h), w=Sb.b(h), eng="act")

                for h0 in range(0, 6, NPAR):
                    gens = [head_ops(h0 + i, HS[i], HBANKS[i][0], HBANKS[i][1]) for i in range(NPAR)]
                    while gens:
                        for g_ in list(gens):
                            try:
                                next(g_)
                            except StopIteration:
                                gens.remove(g_)
                k.act(rsO.ap, ssO.ap, AF.Identity, bias=epsb.ap[:, 0:1], scale=1.0 / 128, r=[ssO, epsb], w=[rsO])
                k.tt(rsO.ap, rsO.ap, mhalf.ap[:, 0:6], ALU.pow, r=[rsO, mhalf], w=[rsO], eng="pool")
                zz = ztok.ap[:, j, 0:768]
                k.act(tz.ap, zz, AF.Tanh, scale=0.5, r=ztok.b(j), w=[tz])
                k.stt(tz.ap, tz.ap, 1.0, zz, ALU.add, ALU.mult, r=[tz] + ztok.b(j), w=[tz])
                tz3 = tz.ap.rearrange("p (h d) -> p h d", h=6)
                gz3 = gz.ap.rearrange("p (h d) -> p h d", h=6)
                k.tt(gz3, tz3, ngb.ap.unsqueeze(1).to_broadcast([128, 6, 128]), ALU.mult, r=[tz, ngb], w=[gz])
                k.tt(gz3, gz3, rsO.ap.unsqueeze(2).to_broadcast([128, 6, 128]), ALU.mult, r=[gz, rsO], w=[gz])
                k.tt(mix_tok.ap[:, 0:768].rearrange("p (h d) -> p h d", h=6), Ot.ap, gz3, ALU.mult, r=[Ot, gz], w=[mix_tok])
                if cfg.gstop <= 14:
                    continue
                mq = TL(mqT.ap[:, :, :, cs_])
                mq.bufs = mqT.bufs
                mem_attn(l, blk, mq, mix_tok, S, 0, [PS[1], PS[2]])
                ro = Ro[blk % 2]
                out_proj_block(mix_tok, mixT, wo, ro.ap, ro.b(), blk)
        k.dma(S_out[a], Sf.ap, r=[Sf], w=[carry_buf])
        k.dma(hist_out[a], hist.ap, r=[hist], w=[carry_buf])
        P.barrier()
        A.release(m0)

    def init_phase():
        m0 = A.mark()
        t = [A.alloc((D,), F32) for _ in range(2)]
        for b in range(NB):
            k.dma(t[b % 2].ap, x[b * 128:(b + 1) * 128, :], w=[t[b % 2]])
            k.dma(Rd[b * 128:(b + 1) * 128, :], t[b % 2].ap, r=[t[b % 2]], w=[Rbuf[b]])
        P.barrier()
        A.release(m0)

    setup_phase()
    init_phase()
    nl = len(cfg.layers)
    for li, l in enumerate(cfg.layers):
        if l >= 2 and (li == 0 or cfg.layers[li - 1] < 2):
            kv_phase()
        if l < 2:
            gdn_phase(l)
        elif not cfg.skip_mixer:
            swa_phase(l)
        if not cfg.skip_ffn:
            ffn_phase(l, li == nl - 1)
    if cfg.skip_ffn:
        fin_phase()

    P.finalize(nc)
    st.close()
    return nc, P


INPUT_NAMES = ["ln_mix", "ln_ffn", "ln_mem", "w_mem_kv", "w_out", "w_gate_up", "w_down", "gdn_w_in", "gdn_conv",
               "gdn_A_log", "gdn_dt_bias", "gdn_norm", "swa_w_q", "swa_sinks", "ln_kv", "w_kv", "ln_final"]


import ml_dtypes

SEG = 4096


def _zeros_carry():
    return {"s_in": np.zeros((2, 128, 6, 128), np.float32), "hist_in": np.zeros((2, 128, 18, 4), np.float32),
            "kt_in": np.zeros((128, 128), ml_dtypes.bfloat16), "vt_in": np.zeros((128, 128), ml_dtypes.bfloat16),
            "hv": np.zeros((1,), np.float32)}


def run(cfg, inputs, ncores_batch, nseg=1):
    nc, P = build(cfg)
    shared = {n: np.ascontiguousarray(np.asarray(inputs[n], dtype=np.float32)) for n in INPUT_NAMES}
    carry = [_zeros_carry() for _ in ncores_batch]
    outs = [[] for _ in ncores_batch]
    for sgi in range(nseg):
        in_maps = []
        for ci, (b, t0) in enumerate(ncores_batch):
            ts = t0 + sgi * cfg.ntok
            m = dict(shared)
            m["x"] = np.ascontiguousarray(np.asarray(inputs["x"])[b, ts:ts + cfg.ntok, :], dtype=np.float32)
            m["mem"] = np.ascontiguousarray(np.asarray(inputs["mem"])[b], dtype=np.float32)
            m["pos"] = np.ascontiguousarray(np.asarray(inputs["positions"])[b, ts:ts + cfg.ntok], dtype=np.int32)
            m.update(carry[ci])
            in_maps.append(m)
        res = run_bass_kernel_spmd(nc, in_maps, core_ids=list(range(len(in_maps))))
        for ci, r in enumerate(res.results):
            outs[ci].append(r["out"])
            carry[ci] = {"s_in": np.asarray(r["s_out"]), "hist_in": np.asarray(r["hist_out"]),
                         "kt_in": np.asarray(r["kt_out"]), "vt_in": np.asarray(r["vt_out"]),
                         "hv": np.ones((1,), np.float32)}
    return [np.concatenate(o, axis=0) for o in outs]


def kernel(**inputs):
    B, S, _ = inputs["x"].shape
    cfg = Cfg(SEG)
    outs = run(cfg, inputs, [(b, 0) for b in range(B)], nseg=S // SEG)
    return np.stack(outs, axis=0).astype(np.float32)
```
